# Optimizing a Trainium2 kernel written in Bass

```python
import math
import jax, jax.numpy as jnp
from jax import lax
import numpy as np

D_MODEL = 1024
BATCH = 8
SEQ = 8192
DEPTH = 2

MIX_WIDTH = D_MODEL
CONV_CH = MIX_WIDTH // 2
CONV_WIDTH = 31
SG_CH = MIX_WIDTH - CONV_CH
SG_HEADS = 4
SG_HEAD_DIM = SG_CH // SG_HEADS
SG_CHUNK = 128
IN_AB = 2 * CONV_CH + 2 * SG_CH
DIFF_HEAD_DIM = 64
DIFF_HEADS = D_MODEL // (2 * DIFF_HEAD_DIM)
Q_BLOCK = 128
N_BUCKETS = 32
MAX_DISTANCE = 128
D_FF = 4 * D_MODEL
N_EVEN = (DEPTH + 1) // 2
N_ODD = DEPTH // 2
RMS_EPS = 1e-6
LN_EPS = 1e-5
SUBLN_EPS = 1e-5

kernel_name = "hybrid_conv_gmlp_diffattn_adaln_encoder"


def rmsnorm(x, g, eps=RMS_EPS):
    xf = x.astype(jnp.float32)
    y = xf * lax.rsqrt(jnp.mean(xf * xf, axis=-1, keepdims=True) + eps)
    return (y * g.astype(jnp.float32)).astype(x.dtype)


def layernorm(x, g, b, eps=LN_EPS):
    xf = x.astype(jnp.float32)
    mu = jnp.mean(xf, axis=-1, keepdims=True)
    xc = xf - mu
    var = jnp.mean(xc * xc, axis=-1, keepdims=True)
    return (xc * lax.rsqrt(var + eps) * g.astype(jnp.float32) + b.astype(jnp.float32)).astype(x.dtype)


def modulate(h, shift, scale):
    return h * (1.0 + scale[:, None, :]) + shift[:, None, :]


def t5_bucket(rel):
    nb = N_BUCKETS // 2
    max_exact = nb // 2
    ret = jnp.where(rel > 0, nb, 0).astype(jnp.int32)
    n = jnp.abs(rel)
    nf = jnp.maximum(n, 1).astype(jnp.float32)
    large = max_exact + (jnp.log(nf / max_exact) / math.log(MAX_DISTANCE / max_exact)
                         * (nb - max_exact)).astype(jnp.int32)
    large = jnp.minimum(large, nb - 1)
    return ret + jnp.where(n < max_exact, n, large)


def conv_gmlp_mixer(h, w_in, conv_w, conv_b, conv_ln_g, conv_ln_b,
                    sg_ln_g, sg_ln_b, sg_w, sg_b, w_out):
    B, S, _ = h.shape
    z = h @ w_in
    a_val = z[..., :CONV_CH]
    a_gate = z[..., CONV_CH:2 * CONV_CH]
    z_sg = z[..., 2 * CONV_CH:]
    a = a_val * jax.nn.sigmoid(a_gate)
    pad = CONV_WIDTH // 2
    a = lax.conv_general_dilated(
        a, conv_w[:, None, :].astype(a.dtype), window_strides=(1,),
        padding=[(pad, pad)], dimension_numbers=('NWC', 'WIO', 'NWC'),
        feature_group_count=CONV_CH) + conv_b
    a = jax.nn.silu(layernorm(a, conv_ln_g, conv_ln_b))
    zs = jax.nn.gelu(z_sg, approximate=False)
    u = zs[..., :SG_CH]
    v = layernorm(zs[..., SG_CH:], sg_ln_g, sg_ln_b)
    v = v.reshape(B, S // SG_CHUNK, SG_CHUNK, SG_HEADS, SG_HEAD_DIM)
    v = jnp.einsum('gpq,bnqgc->bnpgc', sg_w, v) + sg_b.T[None, None, :, :, None]
    sg = u * v.reshape(B, S, SG_CH)
    return jnp.concatenate([a, sg], axis=-1) @ w_out


def diff_attention(h, w_qkv, lq1, lk1, lq2, lk2, subln_g, w_out, rel_bias, lambda_init):
    B, S, D = h.shape
    H, d = DIFF_HEADS, DIFF_HEAD_DIM
    q, k, v = jnp.split(h @ w_qkv, 3, axis=-1)
    q = q.reshape(B, S, H, 2, d) * (d ** -0.5)
    k = k.reshape(B, S, H, 2, d)
    v = v.reshape(B, S, H, 2 * d)
    f32 = jnp.float32
    lam = (jnp.exp(jnp.sum(lq1.astype(f32) * lk1.astype(f32)))
           - jnp.exp(jnp.sum(lq2.astype(f32) * lk2.astype(f32))) + lambda_init)
    nblk = S // Q_BLOCK
    q_blocks = jnp.moveaxis(q.reshape(B, nblk, Q_BLOCK, H, 2, d), 1, 0)
    key_pos = jnp.arange(S, dtype=jnp.int32)

    def block(args):
        qb, blk = args
        q_pos = blk * Q_BLOCK + jnp.arange(Q_BLOCK, dtype=jnp.int32)
        rel_idx = key_pos[None, :] - q_pos[:, None] + (S - 1)
        bias = jnp.transpose(jnp.take(rel_bias, rel_idx, axis=0), (2, 0, 1))
        logits = (jnp.einsum('bqhjd,bkhjd->bhjqk', qb, k).astype(f32)
                  + bias.astype(f32)[None, :, None])
        p = jax.nn.softmax(logits, axis=-1)
        attn = p[:, :, 0] - lam * p[:, :, 1]
        return jnp.einsum('bhqk,bkhe->bqhe', attn.astype(v.dtype), v)

    o = lax.map(block, (q_blocks, jnp.arange(nblk, dtype=jnp.int32)))
    o = jnp.moveaxis(o, 0, 1).reshape(B, S, H, 2 * d)
    o = rmsnorm(o, subln_g, SUBLN_EPS) * (1.0 - lambda_init)
    return o.reshape(B, S, D) @ w_out


def setup_inputs(seed: int = 0) -> dict:
    key = jax.random.key(seed)
    ks = iter(jax.random.split(key, 40))
    nrm = lambda shape, s: jax.random.normal(next(ks), shape, jnp.float32) * s
    gain = lambda shape: 1.0 + nrm(shape, 0.02)
    D = D_MODEL
    return {
        "x": nrm((BATCH, SEQ, D), 1.0),
        "c": nrm((BATCH, D), 1.0),
        "mix_norm_g": gain((DEPTH, D)),
        "mix_mod_w": nrm((DEPTH, D, 3 * D), 0.5 * D ** -0.5),
        "mix_mod_b": nrm((DEPTH, 3 * D), 0.02),
        "ab_w_in": nrm((N_EVEN, D, IN_AB), D ** -0.5),
        "conv_w": nrm((N_EVEN, CONV_WIDTH, CONV_CH), CONV_WIDTH ** -0.5),
        "conv_b": nrm((N_EVEN, CONV_CH), 0.02),
        "conv_ln_g": gain((N_EVEN, CONV_CH)),
        "conv_ln_b": nrm((N_EVEN, CONV_CH), 0.02),
        "sg_ln_g": gain((N_EVEN, SG_CH)),
        "sg_ln_b": nrm((N_EVEN, SG_CH), 0.02),
        "sg_w": nrm((N_EVEN, SG_HEADS, SG_CHUNK, SG_CHUNK), SG_CHUNK ** -0.5),
        "sg_b": nrm((N_EVEN, SG_HEADS, SG_CHUNK), 0.02),
        "ab_w_out": nrm((N_EVEN, MIX_WIDTH, D), MIX_WIDTH ** -0.5),
        "attn_w_qkv": nrm((N_ODD, D, 3 * D), D ** -0.5),
        "lam_q1": nrm((N_ODD, DIFF_HEAD_DIM), 0.1),
        "lam_k1": nrm((N_ODD, DIFF_HEAD_DIM), 0.1),
        "lam_q2": nrm((N_ODD, DIFF_HEAD_DIM), 0.1),
        "lam_k2": nrm((N_ODD, DIFF_HEAD_DIM), 0.1),
        "subln_g": gain((N_ODD, 2 * DIFF_HEAD_DIM)),
        "attn_w_out": nrm((N_ODD, D, D), D ** -0.5),
        "rel_bias_table": nrm((N_BUCKETS, DIFF_HEADS), 0.2),
        "mlp_norm_g": gain((DEPTH, D)),
        "mlp_mod_w": nrm((DEPTH, D, 3 * D), 0.5 * D ** -0.5),
        "mlp_mod_b": nrm((DEPTH, 3 * D), 0.02),
        "mlp_w1": nrm((DEPTH, D, D_FF), D ** -0.5),
        "mlp_w2": nrm((DEPTH, D_FF, D), D_FF ** -0.5),
        "final_norm_g": gain((D,)),
    }


def reference(x, c, mix_norm_g, mix_mod_w, mix_mod_b, ab_w_in, conv_w, conv_b,
              conv_ln_g, conv_ln_b, sg_ln_g, sg_ln_b, sg_w, sg_b, ab_w_out,
              attn_w_qkv, lam_q1, lam_k1, lam_q2, lam_k2, subln_g, attn_w_out,
              rel_bias_table, mlp_norm_g, mlp_mod_w, mlp_mod_b, mlp_w1, mlp_w2,
              final_norm_g):
    S = x.shape[1]
    c_act = jax.nn.silu(c)
    rel = jnp.arange(-(S - 1), S, dtype=jnp.int32)
    rel_bias = rel_bias_table[t5_bucket(rel)]
    for i in range(DEPTH):
        j = i // 2
        shift, scale, gate = jnp.split(c_act @ mix_mod_w[i] + mix_mod_b[i], 3, axis=-1)
        hdn = modulate(rmsnorm(x, mix_norm_g[i]), shift, scale)
        if i % 2 == 0:
            y = conv_gmlp_mixer(hdn, ab_w_in[j], conv_w[j], conv_b[j], conv_ln_g[j],
                                conv_ln_b[j], sg_ln_g[j], sg_ln_b[j], sg_w[j], sg_b[j],
                                ab_w_out[j])
        else:
            lambda_init = 0.8 - 0.6 * math.exp(-0.3 * i)
            y = diff_attention(hdn, attn_w_qkv[j], lam_q1[j], lam_k1[j], lam_q2[j],
                               lam_k2[j], subln_g[j], attn_w_out[j], rel_bias, lambda_init)
        x = x + gate[:, None, :] * y
        shift, scale, gate = jnp.split(c_act @ mlp_mod_w[i] + mlp_mod_b[i], 3, axis=-1)
        hdn = modulate(rmsnorm(x, mlp_norm_g[i]), shift, scale)
        x = x + gate[:, None, :] * (jnp.square(jax.nn.relu(hdn @ mlp_w1[i])) @ mlp_w2[i])
    return rmsnorm(x, final_norm_g)
```

```python
import math
import numpy as np
import concourse.bass as bass
import concourse.mybir as mybir
from concourse.bass_utils import run_bass_kernel_spmd

F32 = mybir.dt.float32
BF16 = mybir.dt.bfloat16
AF = mybir.ActivationFunctionType
ALU = mybir.AluOpType
AX = mybir.AxisListType

S = 8192
D = 1024
KC = 8
NT = 512
NTILES = S // NT
H = 8
DFF = 4096
RMS_EPS = 1e-6
LN_EPS = 1e-5
LAMBDA_INIT = 0.8 - 0.6 * math.exp(-0.3 * 1)
NDELTA = 6
STORE_Q = "pool"

PV = {}
_o = 0
for _n, _w in (("c", 8), ("mixg0", 8), ("mlpg0", 8), ("mixg1", 8), ("mlpg1", 8), ("fing", 8),
               ("modb0", 24), ("modb1", 24), ("modb2", 24), ("modb3", 24),
               ("convb", 4), ("clng", 4), ("clnb", 4), ("sgg", 4), ("sublng", 1), ("convw", 124)):
    PV[_n] = _o
    _o += _w
NPV = _o
BC = {"lam": 0, "sgbb": 256, "bmat": 768, "cb": 1280}
NBC = 1280 + 16


class Sem:
    __slots__ = ("h", "count")

    def __init__(self, h):
        self.h = h
        self.count = 0


class Buf:
    __slots__ = ("name", "w", "r", "sem")

    def __init__(self, name=""):
        self.name = name
        self.w = None
        self.r = []
        self.sem = None


class _FirstRec:
    def __init__(self, e):
        self.e = e
        self.first = None

    def __getattr__(self, name):
        f = getattr(self.e, name)

        def w(*a, **k):
            r = f(*a, **k)
            if self.first is None:
                self.first = r
            return r
        return w


class Ctx:
    def __init__(self, nc):
        self.nc = nc
        self.engs = {"pe": nc.tensor, "act": nc.scalar, "dve": nc.vector,
                     "pool": nc.gpsimd, "sp": nc.sync}
        self.prog = {e: [] for e in self.engs}
        self.esem = {e: Sem(nc.alloc_semaphore("es_" + e)) for e in ("pe", "act", "dve", "pool")}
        self.seen = {e: {} for e in self.engs}
        self.free_sems = []
        self.all_dsems = []

    def dsem(self):
        if self.free_sems:
            return self.free_sems.pop()
        s = Sem(self.nc.alloc_semaphore("ds%d" % len(self.all_dsems)))
        self.all_dsems.append(s)
        return s

    def release(self, bufs):
        for b in bufs:
            if b.sem is not None:
                self.free_sems.append(b.sem)
                b.sem = None

    def _deps(self, eng, reads, writes):
        deps = {}
        for b in reads:
            if b.w is not None:
                s, v = b.w
                if deps.get(s, 0) < v:
                    deps[s] = v
        for b in writes:
            if b.w is not None:
                s, v = b.w
                if deps.get(s, 0) < v:
                    deps[s] = v
            for (s, v) in b.r:
                if deps.get(s, 0) < v:
                    deps[s] = v
        waits = []
        seen = self.seen[eng]
        own = self.esem.get(eng)
        for s, v in deps.items():
            if seen.get(s, 0) >= v:
                continue
            if s is own and (eng == "pe" or s.count - v >= 4):
                continue
            seen[s] = v
            waits.append((s.h, v))
        return waits

    def op(self, eng, fn, reads=(), writes=()):
        waits = self._deps(eng, reads, writes)
        s = self.esem[eng]
        s.count += 1
        done = (s, s.count)
        for b in reads:
            b.r.append(done)
        for b in writes:
            b.w = done
            b.r = []
        self.prog[eng].append((waits, fn, s.h, 1))

    def dma(self, q, out_ap, in_ap, reads, writes, sembuf):
        waits = self._deps(q, reads, writes)
        if sembuf.sem is None:
            sembuf.sem = self.dsem()
        s = sembuf.sem
        s.count += 16
        done = (s, s.count)
        for b in reads:
            b.r.append(done)
        for b in writes:
            b.w = done
            b.r = []

        def fn(e, out_ap=out_ap, in_ap=in_ap):
            return e.dma_start(out=out_ap, in_=in_ap)
        self.prog[q].append((waits, fn, s.h, 16))

    def barrier(self):
        targets = [(s, s.count) for s in self.esem.values() if s.count > 0]
        targets += [(s, s.count) for s in self.all_dsems if s.count > 0]
        for e in self.engs:
            seen = self.seen[e]
            waits = []
            for s, v in targets:
                if seen.get(s, 0) >= v:
                    continue
                seen[s] = v
                waits.append((s.h, v))
            if waits:
                self.prog[e].append((waits, None, None, 0))

    def emit(self):
        nc = self.nc
        with nc.Block() as block:
            def mk(name):
                def body(e):
                    for waits, fn, sh, inc in self.prog[name]:
                        if fn is None:
                            for (h, v) in waits:
                                e.wait_ge(h, v)
                            continue
                        for (h, v) in waits:
                            e.wait_ge(h, v)
                        r = fn(e)
                        last = r[1] if isinstance(r, tuple) else r
                        last.then_inc(sh, inc)
                return body
            block.sync(mk("sp"))
            block.scalar(mk("act"))
            block.vector(mk("dve"))
            block.gpsimd(mk("pool"))
            block.tensor(mk("pe"))


class Arena:
    def __init__(self, nc, nbytes):
        self.n = nbytes // 2
        self.t = nc.alloc_sbuf_tensor("arena", [128, self.n], BF16)
        self.off = 0

    def reset(self):
        self.off = 0

    def alloc(self, shape, dt):
        n = int(np.prod(shape))
        units = n * (2 if dt == F32 else 1)
        self.off = (self.off + 15) // 16 * 16
        assert self.off + units <= self.n, ("arena overflow", self.off, units, self.n)
        ap = self.t[:, self.off:self.off + units]
        self.off += units
        if dt == F32:
            ap = ap.bitcast(F32)
        if len(shape) == 2:
            ap = ap.rearrange("p (a b) -> p a b", a=shape[0])
        elif len(shape) == 3:
            ap = ap.rearrange("p (a b c) -> p a b c", a=shape[0], b=shape[1])
        return ap, Buf()


class Rot:
    def __init__(self, items):
        self.items = items
        self.i = 0

    def next(self):
        it = self.items[self.i % len(self.items)]
        self.i += 1
        return it


ALL_PHASES = ("p1", "p2a", "p2b", "p3", "p4", "p6a", "p6b")
SCRATCH = {
    "mixT": ([D, S], BF16, "p1"),
    "x1T": ([D, S], F32, "p2a"),
    "xaT": ([D, S], F32, "p2a"),
    "x2T": ([D, S], F32, "p2b"),
    "qT": ([D, S], BF16, "p3"),
    "kT": ([D, S], BF16, "p3"),
    "Vs": ([H, 128, S // 128, 128], BF16, "p3"),
    "oT": ([D, S], BF16, "p4"),
    "x3T": ([D, S], F32, "p6a"),
    "xbT": ([D, S], F32, "p6a"),
}
CONSUMES = {
    "p1": [], "p2a": ["mixT"], "p2b": ["x1T", "xaT"], "p3": ["x2T"], "p4": ["qT", "kT", "Vs"],
    "p6a": ["oT", "x2T"], "p6b": ["x3T", "xbT"],
}


def build(phases=ALL_PHASES, ntiles=NTILES, nheads=H, debug_out=False):
    nc = bass.Bass("TRN2", target_bir_lowering=False)
    phases = tuple(phases)

    def din(name, shape, dt=F32):
        return nc.dram_tensor(name, list(shape), dt, kind="ExternalInput").ap()

    xT = din("xT", [D, S])
    pv_d = din("pv", [128, NPV])
    bc_d = din("bc", [128, NBC])
    sgwT_d = din("sgwT", [128, 512])
    ident_d = din("ident", [128, 128])
    modw_d = [din("modw%d" % i, [D, 3 * D]) for i in range(4)]
    w_in_d = din("w_in", [D, 2048])
    w_out_d = din("w_out", [D, D])
    wqkv_d = din("wqkv", [D, 3 * D])
    wao_d = din("wao", [D, D])
    w1_d = [din("w1_%d" % i, [D, DFF]) for i in range(2)]
    w2_d = [din("w2_%d" % i, [DFF, D]) for i in range(2)]
    bt_d = din("bt", [H, NDELTA, 128, 512])
    scr = {}
    for name, (shape, dt, prod) in SCRATCH.items():
        produced = prod in phases
        consumed = any(name in CONSUMES[p] for p in phases)
        if produced:
            later = any(name in CONSUMES[p] for p in ALL_PHASES if p not in phases)
            kind = "ExternalOutput" if (later or debug_out) else "Internal"
        elif consumed:
            kind = "ExternalInput"
        else:
            continue
        scr[name] = nc.dram_tensor(name, list(shape), dt, kind=kind).ap()
    outT = None
    if "p6b" in phases:
        outT = nc.dram_tensor("outT", [D, S], F32, kind="ExternalOutput").ap()

    c = Ctx(nc)
    dbg_bufs = []

    def dbg_dump(name, ap, buf, dt=F32):
        if not debug_out:
            return
        shp = [int(x) for x in ap.shape]
        d_ = nc.dram_tensor("dbg_" + name, shp, dt, kind="ExternalOutput").ap()
        b_ = Buf()
        c.dma("sp", d_, ap, [buf], [], b_)
        dbg_bufs.append(b_)
    pv = nc.alloc_sbuf_tensor("pv_sb", [128, NPV], F32)
    bcs = nc.alloc_sbuf_tensor("bc_sb", [128, NBC], F32)
    modv = nc.alloc_sbuf_tensor("modv", [128, 96], F32)
    coefA = nc.alloc_sbuf_tensor("coefA", [128, 32], F32)
    cact = nc.alloc_sbuf_tensor("cact", [128, 8], F32)
    ones_bf = nc.alloc_sbuf_tensor("ones_bf", [128, 128], BF16)
    ones_f = nc.alloc_sbuf_tensor("ones_f", [128, 128], F32)
    b_onesf = Buf()
    ident_f = nc.alloc_sbuf_tensor("ident_f", [128, 128], F32)
    ident_bf = nc.alloc_sbuf_tensor("ident_bf", [128, 128], BF16)
    small = nc.alloc_sbuf_tensor("small", [128, 16], F32)
    lamp = nc.alloc_sbuf_tensor("lamp", [128, 128], F32)
    b_pv, b_bc, b_modv, b_coef, b_cact, b_ones, b_identf, b_identb, b_small, b_lamp = [Buf() for _ in range(10)]
    arena = Arena(nc, 199 * 1024)
    pp = [nc.alloc_psum_tensor("pp%d" % i, [128, 2, 512], F32) for i in range(4)]
    bank = [pp[i // 2][:, i % 2, :] for i in range(8)]
    bankb = [Buf("bank%d" % i) for i in range(8)]

    GATE = lambda s, m: modv[:, s * 24 + 16 + m: s * 24 + 17 + m]
    SHIFT = lambda s, m: modv[:, s * 24 + m: s * 24 + m + 1]
    COEF = lambda s, m: coefA[:, s * 8 + m: s * 8 + m + 1]
    NORMG = {0: "mixg0", 1: "mlpg0", 2: "mixg1", 3: "mlpg1"}

    c.dma("sp", pv[:], pv_d, [], [b_pv], b_pv)
    c.dma("sp", bcs[:], bc_d, [], [b_bc], b_bc)
    c.dma("sp", ident_f[:], ident_d, [], [b_identf], b_identf)
    c.op("pool", lambda e: e.memset(ones_bf[:], 1.0), [], [b_ones])
    c.op("pool", lambda e: e.memset(ones_f[:], 1.0), [], [b_onesf])
    c.op("pool", lambda e: e.tensor_copy(out=ident_bf[:], in_=ident_f[:]), [b_identf], [b_identb])
    c.op("act", lambda e: e.activation(out=cact[:], in_=pv[:, PV["c"]:PV["c"] + 8], func=AF.Silu), [b_pv], [b_cact])
    L = BC["lam"]
    c.op("dve", lambda e: e.tensor_tensor(out=lamp[:, 0:64], in0=bcs[:, L:L + 64], in1=bcs[:, L + 64:L + 128], op=ALU.mult), [b_bc], [b_lamp])
    c.op("dve", lambda e: e.tensor_tensor(out=lamp[:, 64:128], in0=bcs[:, L + 128:L + 192], in1=bcs[:, L + 192:L + 256], op=ALU.mult), [b_bc], [b_lamp])
    c.op("dve", lambda e: e.tensor_reduce(out=small[:, 0:2], in_=lamp[:].rearrange("p (a b) -> p a b", a=2), axis=AX.X, op=ALU.add), [b_lamp], [b_small])
    c.op("act", lambda e: e.activation(out=small[:, 2:4], in_=small[:, 0:2], func=AF.Exp), [b_small], [b_small])
    c.op("dve", lambda e: e.tensor_tensor(out=small[:, 4:5], in0=small[:, 3:4], in1=small[:, 2:3], op=ALU.subtract), [b_small], [b_small])
    c.op("dve", lambda e: e.tensor_scalar(out=small[:, 4:5], in0=small[:, 4:5], scalar1=-LAMBDA_INIT, scalar2=None, op0=ALU.add), [b_small], [b_small])
    c.op("dve", lambda e: e.tensor_scalar(out=small[:, 5:6], in0=pv[:, PV["sublng"]:PV["sublng"] + 1], scalar1=1.0 - LAMBDA_INIT, scalar2=None, op0=ALU.mult), [b_pv, b_small], [b_small])
    NEGLAM = small[:, 4:5]
    SUBLNG = small[:, 5:6]

    arena.reset()
    wst = [arena.alloc((8, 1536), F32) for _ in range(2)]
    mps = pp[0][:, 0, 0:96]
    for s in range(4):
        for half in range(2):
            wb, wbuf = wst[(s * 2 + half) % 2]
            src = modw_d[s][:, half * 1536:(half + 1) * 1536].rearrange("(k p) n -> p k n", p=128)
            c.dma("sp", wb, src, [], [wbuf], wbuf)

            def f(e, wb=wb, s=s, half=half):
                ins = None
                for j in range(12):
                    col = s * 24 + half * 12 + j
                    for k in range(KC):
                        ins = e.matmul(mps[:, col:col + 1], wb[:, k, j * 128:(j + 1) * 128], cact[:, k:k + 1],
                                       start=(k == 0), stop=(k == KC - 1))
                return ins
            c.op("pe", f, [wbuf, b_cact], [bankb[0]])
    for s in range(4):
        o = PV["modb%d" % s]
        c.op("dve", lambda e, s=s, o=o: e.tensor_tensor(out=modv[:, s * 24:(s + 1) * 24], in0=mps[:, s * 24:(s + 1) * 24],
                                                       in1=pv[:, o:o + 24], op=ALU.add), [bankb[0], b_pv], [b_modv])
        g = PV[NORMG[s]]
        c.op("dve", lambda e, s=s, g=g: e.scalar_tensor_tensor(out=coefA[:, s * 8:(s + 1) * 8], in0=modv[:, s * 24 + 8:s * 24 + 16],
                                                              scalar=1.0, in1=pv[:, g:g + 8], op0=ALU.add, op1=ALU.mult),
             [b_modv, b_pv], [b_coef])
    c.barrier()
    c.release([wst[0][1], wst[1][1]])

    stage_rot_engines = ("pool", "dve")

    def load_w_bf16(dst, src, kc, ncols, stages, cnt=[0]):
        step = 2048
        for k in range(kc):
            for n0 in range(0, ncols, step):
                w = min(step, ncols - n0)
                st, sb = stages[cnt[0] % len(stages)]
                eng = stage_rot_engines[cnt[0] % 2]
                cnt[0] += 1
                c.dma("sp", st[:, 0:w], src[k * 128:(k + 1) * 128, n0:n0 + w], [], [sb], sb)
                tmpb = Buf()
                c.op(eng, lambda e, st=st, k=k, n0=n0, w=w: e.tensor_copy(out=dst[:, k, n0:n0 + w], in_=st[:, 0:w]), [sb], [tmpb])

    def rstd_from_psum(ps_ap, rstd_ap, n, scale, eps, reads, writes):
        c.op("act", lambda e: e.activation(out=rstd_ap, in_=ps_ap, func=AF.Ln, scale=scale, bias=eps), reads, writes)
        c.op("act", lambda e: e.activation(out=rstd_ap, in_=rstd_ap, func=AF.Exp, scale=-0.5), writes, writes)

    def norm_h(x_ap, xb, W, s, sq, sqb, rstd, rstdb, tts, h_ap, hb, psrot):
        c.op("act", lambda e: e.activation(out=sq[:, :, 0:W], in_=x_ap, func=AF.Square), [xb], [sqb])
        groups = [(0, min(512, W))] + ([(512, W)] if W > 512 else [])
        for (c0, c1) in groups:
            pi = psrot.next()
            n = c1 - c0

            def f(e, pi=pi, c0=c0, c1=c1, n=n):
                ins = None
                for k in range(KC):
                    ins = e.matmul(bank[pi][:, 0:n], ones_bf[:], sq[:, k, c0:c1], start=(k == 0), stop=(k == KC - 1))
                return ins
            c.op("pe", f, [sqb, b_ones], [bankb[pi]])
            rstd_from_psum(bank[pi][:, 0:n], rstd[:, c0:c1], n, 1.0 / D, RMS_EPS, [bankb[pi]], [rstdb])
        for k in range(KC):
            t, tb = tts.next()
            c.op("dve", lambda e, t=t, k=k: e.tensor_tensor(out=t[:, 0:W], in0=x_ap[:, k, :], in1=rstd[:, 0:W], op=ALU.mult), [xb, rstdb], [tb])
            c.op("act", lambda e, t=t, k=k: e.activation(out=h_ap[:, k, 0:W], in_=t[:, 0:W], func=AF.Identity,
                                                        scale=COEF(s, k), bias=SHIFT(s, k)), [tb, b_coef, b_modv], [hb])

    def mm_group(pi, n, lhs_list, rhs_list, reads):
        def f(e):
            ins = first = None
            L_ = len(lhs_list)
            for i in range(L_):
                ins = e.matmul(bank[pi][:, 0:n], lhs_list[i], rhs_list[i], start=(i == 0), stop=(i == L_ - 1))
                if first is None:
                    first = ins
            return first, ins
        c.op("pe", f, reads, [bankb[pi]])

    if "p1" in phases:
        arena.reset()
        w_in_sb, _ = arena.alloc((8, 2048), BF16)
        dg, b_dg = arena.alloc((4, 31, 128), BF16)
        sgw_f, b_sgwf = arena.alloc((4, 128), F32)
        sgw_bf, b_sgwb = arena.alloc((4, 128), BF16)
        Rt, b_R = arena.alloc((4, 128), F32)
        mark = arena.off
        stages = [arena.alloc((2048,), F32) for _ in range(3)]
        load_w_bf16(w_in_sb, w_in_d, 8, 2048, stages)
        c.dma("sp", sgw_f, sgwT_d.rearrange("p (a b) -> p a b", a=4), [], [b_sgwf], b_sgwf)
        c.op("pool", lambda e: e.tensor_copy(out=sgw_bf, in_=sgw_f), [b_sgwf], [b_sgwb])
        for cc in range(4):
            o = PV["convw"] + cc * 31
            c.op("dve", lambda e, cc=cc, o=o: e.tensor_tensor(
                out=dg[:, cc, :, :], in0=ident_f[:].unsqueeze(1).broadcast_to([128, 31, 128]),
                in1=pv[:, o:o + 31].unsqueeze(2).broadcast_to([128, 31, 128]), op=ALU.mult), [b_identf, b_pv], [b_dg])
        bm = BC["bmat"]

        def fR(e):
            ins = None
            for g in range(4):
                ins = e.matmul(bank[1][:, g * 128:(g + 1) * 128], bcs[:, bm + g * 128:bm + (g + 1) * 128], sgw_f[:, g, :],
                               start=True, stop=True)
            return ins
        c.op("pe", fR, [b_bc, b_sgwf], [bankb[1]])
        c.op("dve", lambda e: e.tensor_tensor(out=Rt.rearrange("p a b -> p (a b)"), in0=bank[1], in1=bcs[:, BC["sgbb"]:BC["sgbb"] + 512],
                                              op=ALU.add), [bankb[1], b_bc], [b_R])
        c.barrier()
        c.release([st[1] for st in stages])
        arena.off = mark
        WX = 544
        xes = Rot([arena.alloc((8, WX), F32) for _ in range(2)])
        sq, sqb = arena.alloc((8, WX), BF16)
        rstd, rstdb = arena.alloc((WX,), F32)
        tts = Rot([arena.alloc((WX,), F32) for _ in range(3)])
        hs = Rot([arena.alloc((8, WX), BF16) for _ in range(1)])
        a_t, a_b = arena.alloc((4, WX), BF16)
        sgts = Rot([arena.alloc((512,), F32) for _ in range(2)])
        sgh, sghb = arena.alloc((4, 32), F32)
        yb, ybb = arena.alloc((4, 512), BF16)
        ysq, ysqb = arena.alloc((4, 512), BF16)
        mean_t, meanb = arena.alloc((512,), F32)
        msq_t, msqb = arena.alloc((512,), F32)
        rs2, rs2b = arena.alloc((512,), F32)
        dts = Rot([arena.alloc((512,), F32) for _ in range(2)])
        mixeds = Rot([arena.alloc((8, 512), BF16) for _ in range(2)])
        ug, ugb = arena.alloc((4, 512), BF16)
        gvs = Rot([arena.alloc((512,), F32) for _ in range(2)])
        st6, st6b = arena.alloc((8,), F32)
        mv, mvb = arena.alloc((4,), F32)
        vn, vnb = arena.alloc((4, 512), BF16)
        sgtmp = Rot([arena.alloc((512,), F32) for _ in range(2)])
        for (ap_, b_) in xes.items:
            c.op("pool", lambda e, ap_=ap_: e.memset(ap_, 0.0), [], [b_])
        psrot = Rot([0, 1, 2, 3, 4, 5, 6, 7])
        for i in range(ntiles):
            t0 = i * NT
            mixed, mixb = mixeds.next()
            lo, hi = t0 - 16, t0 + NT + 16
            slo, shi = max(lo, 0), min(hi, S)
            xe, xeb = xes.next()
            c.dma("sp", xe[:, :, slo - lo:shi - lo], xT[:, slo:shi].rearrange("(k p) t -> p k t", p=128), [], [xeb], xeb)
            h_ap, hb = hs.next()
            norm_h(xe, xeb, WX, 0, sq, sqb, rstd, rstdb, tts, h_ap, hb, psrot)
            ph = psrot.next()
            for j in range(4):
                pa, pg = psrot.next(), psrot.next()
                mm_group(pa, 512, [w_in_sb[:, k, j * 128:(j + 1) * 128] for k in range(KC)], [h_ap[:, k, 0:512] for k in range(KC)], [hb])
                mm_group(pg, 512, [w_in_sb[:, k, 512 + j * 128:512 + (j + 1) * 128] for k in range(KC)], [h_ap[:, k, 0:512] for k in range(KC)], [hb])
                sgt, sgtb = sgts.next()
                c.op("act", lambda e, pg=pg, sgt=sgt: e.activation(out=sgt, in_=bank[pg], func=AF.Sigmoid), [bankb[pg]], [sgtb])
                c.op("dve", lambda e, pa=pa, sgt=sgt, j=j: e.tensor_tensor(out=a_t[:, j, 0:512], in0=bank[pa], in1=sgt, op=ALU.mult),
                     [bankb[pa], sgtb], [a_b])

            def fh(e, ph=ph, h_ap=h_ap):
                ins = None
                for j in range(4):
                    for g2 in range(2):
                        for k in range(KC):
                            ins = e.matmul(bank[ph][:, j * 64 + g2 * 32:j * 64 + g2 * 32 + 32],
                                           w_in_sb[:, k, g2 * 512 + j * 128:g2 * 512 + (j + 1) * 128], h_ap[:, k, 512:544],
                                           start=(k == 0), stop=(k == KC - 1))
                return ins
            c.op("pe", fh, [hb], [bankb[ph]])
            phv = bank[ph][:, 0:256].rearrange("p (j g n) -> p j g n", j=4, g=2)
            c.op("act", lambda e, phv=phv: e.activation(out=sgh, in_=phv[:, :, 1, :], func=AF.Sigmoid), [bankb[ph]], [sghb])
            c.op("dve", lambda e, phv=phv: e.tensor_tensor(out=a_t[:, :, 512:544], in0=phv[:, :, 0, :], in1=sgh, op=ALU.mult),
                 [bankb[ph], sghb], [a_b])
            if lo < 0:
                c.op("pool", lambda e: e.memset(a_t[:, :, 0:16], 0.0), [], [a_b])
            if hi > S:
                c.op("pool", lambda e: e.memset(a_t[:, :, 528:544], 0.0), [], [a_b])
            for cc in range(4):
                pc = psrot.next()
                mm_group(pc, 512, [dg[:, cc, w, :] for w in range(31)], [a_t[:, cc, w + 1:w + 513] for w in range(31)], [a_b, b_dg])
                cb_ap = pv[:, PV["convb"] + cc:PV["convb"] + cc + 1]
                c.op("act", lambda e, pc=pc, cc=cc, cb_ap=cb_ap: e.activation(out=yb[:, cc, :], in_=bank[pc], func=AF.Identity, bias=cb_ap),
                     [bankb[pc], b_pv], [ybb])
                c.op("act", lambda e, pc=pc, cc=cc, cb_ap=cb_ap: e.activation(out=ysq[:, cc, :], in_=bank[pc], func=AF.Square, bias=cb_ap),
                     [bankb[pc], b_pv], [ysqb])
            p1_, p2_ = psrot.next(), psrot.next()
            mm_group(p1_, 512, [ones_bf[:]] * 4, [yb[:, cc, :] for cc in range(4)], [ybb, b_ones])
            mm_group(p2_, 512, [ones_bf[:]] * 4, [ysq[:, cc, :] for cc in range(4)], [ysqb, b_ones])
            c.op("dve", lambda e, p1_=p1_: e.tensor_scalar(out=mean_t, in0=bank[p1_], scalar1=1.0 / 512, scalar2=None, op0=ALU.mult), [bankb[p1_]], [meanb])
            c.op("dve", lambda e: e.tensor_tensor(out=msq_t, in0=mean_t, in1=mean_t, op=ALU.mult), [meanb], [msqb])
            c.op("dve", lambda e, p2_=p2_: e.scalar_tensor_tensor(out=msq_t, in0=bank[p2_], scalar=1.0 / 512, in1=msq_t, op0=ALU.mult, op1=ALU.subtract),
                 [bankb[p2_], msqb], [msqb])
            c.op("act", lambda e: e.activation(out=rs2, in_=msq_t, func=AF.Ln, bias=LN_EPS), [msqb], [rs2b])
            c.op("act", lambda e: e.activation(out=rs2, in_=rs2, func=AF.Exp, scale=-0.5), [rs2b], [rs2b])
            for cc in range(4):
                dt_, dtb = dts.next()
                c.op("dve", lambda e, dt_=dt_, cc=cc: e.tensor_tensor(out=dt_, in0=yb[:, cc, :], in1=mean_t, op=ALU.subtract), [ybb, meanb], [dtb])
                c.op("dve", lambda e, dt_=dt_: e.tensor_tensor(out=dt_, in0=dt_, in1=rs2, op=ALU.mult), [dtb, rs2b], [dtb])
                g_ap = pv[:, PV["clng"] + cc:PV["clng"] + cc + 1]
                b_ap = pv[:, PV["clnb"] + cc:PV["clnb"] + cc + 1]
                c.op("act", lambda e, dt_=dt_, cc=cc, g_ap=g_ap, b_ap=b_ap, mixed=mixed: e.activation(out=mixed[:, cc, :], in_=dt_, func=AF.Silu, scale=g_ap, bias=b_ap),
                     [dtb, b_pv], [mixb])
            for g in range(4):
                pu = psrot.next()
                mm_group(pu, 512, [w_in_sb[:, k, 1024 + g * 128:1024 + (g + 1) * 128] for k in range(KC)], [h_ap[:, k, 16:528] for k in range(KC)], [hb])
                c.op("act", lambda e, pu=pu, g=g: e.activation(out=ug[:, g, :], in_=bank[pu], func=AF.Gelu), [bankb[pu]], [ugb])
            for n in range(4):
                pvb = psrot.next()
                mm_group(pvb, 512, [h_ap[:, k, 16 + n * 128:16 + (n + 1) * 128] for k in range(KC)], [w_in_sb[:, k, 1536:2048] for k in range(KC)], [hb])
                gv, gvb = gvs.next()
                c.op("act", lambda e, pvb=pvb, gv=gv: e.activation(out=gv, in_=bank[pvb], func=AF.Gelu), [bankb[pvb]], [gvb])
                c.op("dve", lambda e, gv=gv: e.bn_stats(out=st6[:, 0:6], in_=gv), [gvb], [st6b])
                c.op("dve", lambda e: e.bn_aggr(out=mv[:, 0:2], in_=st6[:, 0:6]), [st6b], [mvb])
                c.op("act", lambda e: e.activation(out=mv[:, 2:3], in_=mv[:, 1:2], func=AF.Ln, bias=LN_EPS), [mvb], [mvb])
                c.op("act", lambda e: e.activation(out=mv[:, 2:3], in_=mv[:, 2:3], func=AF.Exp, scale=-0.5), [mvb], [mvb])
                c.op("dve", lambda e, gv=gv, n=n: e.tensor_scalar(out=vn[:, n, :], in0=gv, scalar1=mv[:, 0:1], scalar2=mv[:, 2:3],
                                                               op0=ALU.subtract, op1=ALU.mult), [gvb, mvb], [vnb])
            for g in range(4):
                pm = psrot.next()

                def fsp(e, pm=pm, g=g):
                    ins = None
                    for n in range(4):
                        ins = e.matmul(bank[pm][:, n * 128:(n + 1) * 128], vn[:, n, g * 128:(g + 1) * 128], sgw_bf[:, g, :], start=True, stop=True)
                    return ins
                c.op("pe", fsp, [vnb, b_sgwb], [bankb[pm]])
                tm, tmb = sgtmp.next()
                gg = pv[:, PV["sgg"] + g:PV["sgg"] + g + 1]
                c.op("dve", lambda e, pm=pm, tm=tm, g=g, gg=gg: e.scalar_tensor_tensor(
                    out=tm.rearrange("p (a b) -> p a b", a=4), in0=bank[pm].rearrange("p (a b) -> p a b", a=4), scalar=gg,
                    in1=Rt[:, g, :].unsqueeze(1).broadcast_to([128, 4, 128]), op0=ALU.mult, op1=ALU.add), [bankb[pm], b_R, b_pv], [tmb])
                c.op("pool", lambda e, tm=tm, g=g, mixed=mixed: e.tensor_tensor(out=mixed[:, 4 + g, :], in0=tm, in1=ug[:, g, :], op=ALU.mult), [tmb, ugb], [mixb])
            if i == 0:
                dbg_dump("a0", a_t, a_b, BF16)
                dbg_dump("yb0", yb, ybb, BF16)
                dbg_dump("mean0", mean_t, meanb)
                dbg_dump("var0", msq_t, msqb)
                dbg_dump("rs20", rs2, rs2b)
                dbg_dump("h0", h_ap, hb, BF16)
                dbg_dump("rstd0", rstd, rstdb)
            if i == 1:
                dbg_dump("u1", ug, ugb, BF16)
                dbg_dump("vn1", vn, vnb, BF16)
                dbg_dump("R", Rt, b_R)
                dbg_dump("tm1", tm, tmb)
            c.dma(STORE_Q, scr["mixT"][:, t0:t0 + NT].rearrange("(k p) t -> p k t", p=128), mixed, [mixb], [], mixb)
        c.barrier()
        c.release([b for (_, b) in xes.items] + [b for (_, b) in mixeds.items])

    def mlp_pass(s, layer, half, src_d, res_d, out_d, final=False, pre=None):
        arena.reset()
        w1h, _ = arena.alloc((8, 2048), BF16)
        w2h, _ = arena.alloc((16, 1024), BF16)
        if pre is not None:
            wpre_sb, _ = arena.alloc((8, 1024), BF16)
        mark = arena.off
        stages = [arena.alloc((2048,), F32) for _ in range(3)]
        load_w_bf16(w1h, w1_d[layer][:, half * 2048:(half + 1) * 2048], 8, 2048, stages)
        load_w_bf16(w2h, w2_d[layer][half * 2048:(half + 1) * 2048, :], 16, 1024, stages)
        if pre is not None:
            load_w_bf16(wpre_sb, pre[4], 8, 1024, stages)
        c.barrier()
        c.release([st[1] for st in stages])
        arena.off = mark
        xss = Rot([arena.alloc((8, 512), F32) for _ in range(2)])
        rch = Rot([arena.alloc((512,), F32) for _ in range(4)])
        sq, sqb = arena.alloc((8, 512), BF16)
        rstd, rstdb = arena.alloc((512,), F32)
        tts = Rot([arena.alloc((512,), F32) for _ in range(3)])
        hs = Rot([arena.alloc((8, 512), BF16) for _ in range(2)])
        rts = Rot([arena.alloc((512,), BF16) for _ in range(3)])
        hid, hidb = arena.alloc((16, 512), BF16)
        if final:
            xof, xofb = arena.alloc((8, 512), F32)
        else:
            och = Rot([arena.alloc((512,), F32) for _ in range(4)])
        if pre is not None:
            ots = Rot([arena.alloc((8, 512), BF16) for _ in range(1)])
        psrot = Rot([0, 1, 2, 3, 4, 5, 6, 7])
        tr = lambda d_, t0: d_[:, t0:t0 + NT].rearrange("(k p) t -> p k t", p=128)
        ch = lambda d_, m, t0: d_[m * 128:(m + 1) * 128, t0:t0 + NT]
        for i in range(ntiles):
            t0 = i * NT
            xs, xsb = xss.next()
            if pre is None:
                c.dma("sp", xs, tr(src_d, t0), [], [xsb], xsb)
            else:
                o_d, xres_d, xout_d, sa, _w = pre
                ot, otb = ots.next()
                c.dma("sp", ot, tr(o_d, t0), [], [otb], otb)
                for m in range(KC):
                    rc, rcb = rch.next()
                    c.dma("sp", rc, ch(xres_d, m, t0), [], [rcb], rcb)
                    po = psrot.next()
                    mm_group(po, 512, [wpre_sb[:, k, m * 128:(m + 1) * 128] for k in range(KC)], [ot[:, k, :] for k in range(KC)], [otb])
                    c.op("dve", lambda e, po=po, m=m, xs=xs, rc=rc: e.scalar_tensor_tensor(out=xs[:, m, :], in0=bank[po], scalar=GATE(sa, m), in1=rc,
                                                                                       op0=ALU.mult, op1=ALU.add), [bankb[po], rcb, b_modv], [xsb])
                c.dma(STORE_Q, tr(xout_d, t0), xs, [xsb], [], xsb)
            h_ap, hb = hs.next()
            norm_h(xs, xsb, 512, s, sq, sqb, rstd, rstdb, tts, h_ap, hb, psrot)
            for j in range(16):
                pj = psrot.next()
                mm_group(pj, 512, [w1h[:, k, j * 128:(j + 1) * 128] for k in range(KC)], [h_ap[:, k, :] for k in range(KC)], [hb])
                rt, rtb = rts.next()
                c.op("act", lambda e, pj=pj, rt=rt: e.activation(out=rt, in_=bank[pj], func=AF.Relu), [bankb[pj]], [rtb])
                c.op("dve", lambda e, pj=pj, rt=rt, j=j: e.scalar_tensor_tensor(out=hid[:, j, :], in0=bank[pj], scalar=0.0, in1=rt, op0=ALU.max, op1=ALU.mult),
                     [bankb[pj], rtb], [hidb])
            for m in range(KC):
                if res_d is not None:
                    rc, rcb = rch.next()
                    c.dma("sp", rc, ch(res_d, m, t0), [], [rcb], rcb)
                else:
                    rc, rcb = xs[:, m, :], xsb
                po = psrot.next()
                mm_group(po, 512, [w2h[:, j, m * 128:(m + 1) * 128] for j in range(16)], [hid[:, j, :] for j in range(16)], [hidb])
                if final:
                    oc, ocb = xof[:, m, :], xofb
                else:
                    oc, ocb = och.next()
                c.op("dve", lambda e, po=po, m=m, oc=oc, rc=rc: e.scalar_tensor_tensor(out=oc, in0=bank[po], scalar=GATE(s, m), in1=rc,
                                                                                   op0=ALU.mult, op1=ALU.add), [bankb[po], rcb, b_modv], [ocb])
                if not final:
                    c.dma(STORE_Q, ch(out_d, m, t0), oc, [ocb], [], ocb)
            if final:
                c.op("act", lambda e: e.activation(out=sq, in_=xof, func=AF.Square), [xofb], [sqb])
                pf = psrot.next()
                mm_group(pf, 512, [ones_bf[:]] * KC, [sq[:, k, :] for k in range(KC)], [sqb, b_ones])
                rstd_from_psum(bank[pf], rstd, 512, 1.0 / D, RMS_EPS, [bankb[pf]], [rstdb])
                for m in range(KC):
                    fg = pv[:, PV["fing"] + m:PV["fing"] + m + 1]
                    c.op("dve", lambda e, m=m, fg=fg: e.scalar_tensor_tensor(out=xof[:, m, :], in0=xof[:, m, :], scalar=fg, in1=rstd, op0=ALU.mult, op1=ALU.mult),
                         [xofb, rstdb, b_pv], [xofb])
                c.dma(STORE_Q, tr(out_d, t0), xof, [xofb], [], xofb)
        c.barrier()
        rel = [b for (_, b) in xss.items] + [b for (_, b) in rch.items]
        if final:
            rel.append(xofb)
        else:
            rel += [b for (_, b) in och.items]
        if pre is not None:
            rel += [b for (_, b) in ots.items]
        c.release(rel)

    if "p2a" in phases:
        mlp_pass(1, 0, 0, None, None, scr["xaT"], pre=(scr["mixT"], xT, scr["x1T"], 0, w_out_d))
    if "p2b" in phases:
        mlp_pass(1, 0, 1, scr["x1T"], scr["xaT"], scr["x2T"])

    if "p3" in phases:
        arena.reset()
        wq, _ = arena.alloc((8, 3072), BF16)
        stages = [arena.alloc((2048,), F32) for _ in range(3)]
        load_w_bf16(wq, wqkv_d, 8, 3072, stages)
        c.barrier()
        c.release([st[1] for st in stages])
        xss = Rot([arena.alloc((8, 512), F32) for _ in range(2)])
        sq, sqb = arena.alloc((8, 512), BF16)
        rstd, rstdb = arena.alloc((512,), F32)
        tts = Rot([arena.alloc((512,), F32) for _ in range(3)])
        hs = Rot([arena.alloc((8, 512), BF16) for _ in range(2)])
        qos = Rot([arena.alloc((8, 512), BF16) for _ in range(2)])
        kos = Rot([arena.alloc((8, 512), BF16) for _ in range(2)])
        vos = Rot([arena.alloc((8, 4, 128), BF16) for _ in range(2)])
        psrot = Rot([0, 1, 2, 3, 4, 5, 6, 7])
        tr = lambda d_, t0: d_[:, t0:t0 + NT].rearrange("(k p) t -> p k t", p=128)
        for i in range(ntiles):
            t0 = i * NT
            xs, xsb = xss.next()
            c.dma("sp", xs, tr(scr["x2T"], t0), [], [xsb], xsb)
            h_ap, hb = hs.next()
            norm_h(xs, xsb, 512, 2, sq, sqb, rstd, rstdb, tts, h_ap, hb, psrot)
            qo, qob = qos.next()
            ko, kob = kos.next()
            for m in range(16):
                pq = psrot.next()
                mm_group(pq, 512, [wq[:, k, m * 128:(m + 1) * 128] for k in range(KC)], [h_ap[:, k, :] for k in range(KC)], [hb])
                if m < 8:
                    c.op("act", lambda e, pq=pq, m=m, qo=qo: e.activation(out=qo[:, m, :], in_=bank[pq], func=AF.Copy, scale=0.125), [bankb[pq]], [qob])
                else:
                    c.op("dve", lambda e, pq=pq, m=m, ko=ko: e.tensor_copy(out=ko[:, m - 8, :], in_=bank[pq]), [bankb[pq]], [kob])
            c.dma(STORE_Q, tr(scr["qT"], t0), qo, [qob], [], qob)
            c.dma(STORE_Q, tr(scr["kT"], t0), ko, [kob], [], kob)
            vo, vob = vos.next()
            for n in range(4):
                for hf in range(2):
                    pvv = psrot.next()
                    mm_group(pvv, 512, [h_ap[:, k, n * 128:(n + 1) * 128] for k in range(KC)],
                             [wq[:, k, 2048 + hf * 512:2048 + (hf + 1) * 512] for k in range(KC)], [hb])
                    src = bank[pvv].rearrange("p (a b) -> p a b", a=4)
                    if hf == 0:
                        c.op("act", lambda e, src=src, vo=vo, n=n, hf=hf: e.activation(out=vo[:, hf * 4:(hf + 1) * 4, n, :], in_=src, func=AF.Copy), [bankb[pvv]], [vob])
                    else:
                        c.op("dve", lambda e, src=src, vo=vo, n=n, hf=hf: e.tensor_copy(out=vo[:, hf * 4:(hf + 1) * 4, n, :], in_=src), [bankb[pvv]], [vob])
            for hh in range(H):
                pass
            c.dma(STORE_Q, scr["Vs"][:, :, 4 * i:4 * i + 4, :].rearrange("h p n e -> p h n e"), vo, [vob], [], vob)
        c.barrier()
        c.release([b for (_, b) in xss.items] + [b for (_, b) in qos.items] + [b for (_, b) in kos.items] + [b for (_, b) in vos.items])

    if "p4" in phases:
        arena.reset()
        NKC = S // 128
        kTs = [arena.alloc((2, S), BF16) for _ in range(2)]
        qTs = [arena.alloc((S,), BF16) for _ in range(2)]
        Vhs = [arena.alloc((NKC, 128), BF16) for _ in range(2)]
        bths = [arena.alloc((NDELTA, 512), BF16) for _ in range(2)]
        btst = Rot([arena.alloc((512,), F32) for _ in range(2)])
        pts = Rot([arena.alloc((2, 512), BF16) for _ in range(4)])
        osqs = Rot([arena.alloc((512,), BF16) for _ in range(2)])
        ons = Rot([arena.alloc((512,), BF16) for _ in range(2)])
        for (kt_, kb_) in kTs:
            c.op("pool", lambda e, kt_=kt_: e.memset(kt_, 0.0), [], [kb_])
        srot = Rot([0, 1])
        Sb = [Buf(), Buf()]
        O1, O2, S1, S2 = 4, 5, 6, 7
        kb1s = [Buf(), Buf()]
        cb0 = BC["cb"]
        FIN_DELAY = 14
        pending = []

        def emit_qk(it):
            if "fin" in it:
                si = srot.next()
                it["si"] = si
                Sp = pp[si]
                osq, osqb = it["fin"][6]
                c.op("pe", lambda e: e.matmul(Sp[:, 0, :], ones_bf[:], osq, start=True, stop=True), [osqb, b_ones], [Sb[si]])
                return
            hh, qt, kc, bufs = it["hh"], it["qt"], it["kc"], it["bufs"]
            kTh, kb, kb1, qTh, qb, Vh, vb, bth, bb = bufs
            q0 = qt * NT
            delta = kc - 4 * qt
            near = (-1 <= delta <= NDELTA - 2)
            si = srot.next()
            Sp = pp[si]
            it["si"], it["near"], it["delta"] = si, near, delta

            def fqk(e):
                e.matmul(Sp[:, 0, :], kTh[:, 0, kc * 128:(kc + 1) * 128], qTh[:, q0:q0 + NT], start=True, stop=not near)
                ins = e.matmul(Sp[:, 1, :], kTh[:, 1, kc * 128:(kc + 1) * 128], qTh[:, q0:q0 + NT], start=True, stop=not near)
                if near:
                    e.matmul(Sp[:, 0, :], ident_bf[:], bth[:, delta + 1, :], start=False, stop=True)
                    ins = e.matmul(Sp[:, 1, :], ident_bf[:], bth[:, delta + 1, :], start=False, stop=True)
                return ins
            c.op("pe", fqk, [kb, kb1, qb, bb, b_identb] if near else [kb, kb1, qb], [Sb[si]])

        def emit_exp(it):
            if "fin" in it:
                (s1, s1b), (s2, s2b), (t1, t1b), (t2, t2b), (ot, otb), (rs, rsb), _osq = it["fin"]
                hh, qt = it["hh"], it["qt"]
                q0 = qt * NT
                Sp = pp[it["si"]]
                c.op("act", lambda e: e.activation(out=rs, in_=Sp[:, 0, :], func=AF.Ln, scale=1.0 / 128, bias=LN_EPS), [Sb[it["si"]]], [rsb])
                c.op("act", lambda e: e.activation(out=rs, in_=rs, func=AF.Exp, scale=-0.5), [rsb], [rsb])
                on, onb = ons.next()
                c.op("dve", lambda e: e.scalar_tensor_tensor(out=on, in0=ot, scalar=SUBLNG, in1=rs, op0=ALU.mult, op1=ALU.mult), [otb, rsb, b_small], [onb])
                c.dma(STORE_Q, scr["oT"][hh * 128:(hh + 1) * 128, q0:q0 + NT], on, [onb], [], onb)
                return
            hh, qt, kc, bufs = it["hh"], it["qt"], it["kc"], it["bufs"]
            kTh, kb, kb1, qTh, qb, Vh, vb, bth, bb = bufs
            si, near, delta = it["si"], it["near"], it["delta"]
            Sp = pp[si]
            pt, ptb = pts.next()
            if near:
                bias = 0.0
            else:
                col = cb0 + hh * 2 + (0 if delta < 0 else 1)
                bias = bcs[:, col:col + 1]
            c.op("act", lambda e: e.activation(out=pt.rearrange("p a b -> p (a b)"), in_=Sp[:].rearrange("p a b -> p (a b)"),
                                               func=AF.Exp, bias=bias), [Sb[si], b_bc], [ptb])
            it["pt"] = (pt, ptb)

        def emit_pv(it):
            if "fin" in it:
                return
            hh, qt, kc, bufs = it["hh"], it["qt"], it["kc"], it["bufs"]
            kTh, kb, kb1, qTh, qb, Vh, vb, bth, bb = bufs
            pt, ptb = it["pt"]

            if kc == 0:
                c.op("dve", lambda e: e.tensor_copy(out=bank[S1], in_=pt[:, 0, :]), [ptb], [bankb[S1]])
            else:
                c.op("dve", lambda e: e.tensor_tensor(out=bank[S1], in0=bank[S1], in1=pt[:, 0, :], op=ALU.add), [ptb, bankb[S1]], [bankb[S1]])

            def fpv(e):
                st_, sp_ = (kc == 0), (kc == NKC - 1)
                e.matmul(bank[O1], Vh[:, kc, :], pt[:, 0, :], start=st_, stop=sp_)
                e.matmul(bank[O2], Vh[:, kc, :], pt[:, 1, :], start=st_, stop=sp_)
                return e.matmul(bank[S2], ones_bf[:], pt[:, 1, :], start=st_, stop=sp_)
            c.op("pe", fpv, [ptb, vb, b_ones], [bankb[O1], bankb[O2], bankb[S2]])
            if kc == NKC - 1:
                emit_fin_a(hh, qt)

        fins = Rot([[arena.alloc((512,), F32) for _ in range(6)] for _ in range(2)])

        def emit_fin_a(hh, qt):
            q0 = qt * NT
            fs = list(fins.next())
            (s1, s1b), (s2, s2b), (t1, t1b), (t2, t2b), (ot, otb), (rs, rsb) = fs
            osq, osqb = osqs.next()
            c.op("dve", lambda e: e.tensor_copy(out=osq, in_=bank[S1]), [bankb[S1]], [osqb])
            c.op("dve", lambda e: e.tensor_copy(out=t1, in_=bank[O1]), [bankb[O1]], [t1b])
            c.op("dve", lambda e: e.tensor_copy(out=t2, in_=bank[O2]), [bankb[O2]], [t2b])
            c.op("pe", lambda e: e.matmul(bank[S1], ones_bf[:], osq, start=True, stop=True), [osqb, b_ones], [bankb[S1]])
            c.op("act", lambda e: e.activation(out=s1, in_=bank[S1], func=AF.Ln), [bankb[S1]], [s1b])
            c.op("act", lambda e: e.activation(out=s1, in_=s1, func=AF.Exp, scale=-1.0), [s1b], [s1b])
            c.op("act", lambda e: e.activation(out=s2, in_=bank[S2], func=AF.Ln), [bankb[S2]], [s2b])
            c.op("act", lambda e: e.activation(out=s2, in_=s2, func=AF.Exp, scale=-1.0), [s2b], [s2b])
            c.op("dve", lambda e: e.tensor_tensor(out=t1, in0=t1, in1=s1, op=ALU.mult), [t1b, s1b], [t1b])
            c.op("dve", lambda e: e.tensor_tensor(out=t2, in0=t2, in1=s2, op=ALU.mult), [t2b, s2b], [t2b])
            c.op("dve", lambda e: e.scalar_tensor_tensor(out=ot, in0=t2, scalar=NEGLAM, in1=t1, op0=ALU.mult, op1=ALU.add), [t1b, t2b, b_small], [otb])
            c.op("dve", lambda e: e.tensor_tensor(out=osq, in0=ot, in1=ot, op=ALU.mult), [otb], [osqb])
            pending.append([FIN_DELAY, {"fin": fs + [(osq, osqb)], "hh": hh, "qt": qt}])

        inflight = []

        def step(it):
            if len(inflight) >= 2:
                emit_exp(inflight[0])
            emit_qk(it)
            if len(inflight) >= 2:
                emit_pv(inflight.pop(0))
            inflight.append(it)

        for hh in range(nheads):
            kTh, kb = kTs[hh % 2]
            qTh, qb = qTs[hh % 2]
            Vh, vb = Vhs[hh % 2]
            bth, bb = bths[hh % 2]
            kb1 = kb1s[hh % 2]
            c.dma("sp", kTh[0:64, 0, :], scr["kT"][hh * 128:hh * 128 + 64, :], [], [kb], kb)
            c.dma("sp", kTh[64:128, 1, :], scr["kT"][hh * 128 + 64:(hh + 1) * 128, :], [kb], [kb1], kb1)
            c.dma("sp", qTh, scr["qT"][hh * 128:(hh + 1) * 128, :], [], [qb], qb)
            c.dma("sp", Vh, scr["Vs"][hh], [], [vb], vb)
            for di in range(NDELTA):
                st_, stb_ = btst.next()
                c.dma("sp", st_, bt_d[hh, di], [], [stb_], stb_)
                c.op("pool", lambda e, st_=st_, bth=bth, di=di: e.tensor_copy(out=bth[:, di, :], in_=st_), [stb_], [bb])
            bufs = (kTh, kb, kb1, qTh, qb, Vh, vb, bth, bb)
            for qt in range(ntiles):
                for kc in range(NKC):
                    step({"hh": hh, "qt": qt, "kc": kc, "bufs": bufs})
                    for p_ in pending:
                        p_[0] -= 1
                    if pending and pending[0][0] <= 0:
                        step(pending.pop(0)[1])
        while True:
            if pending:
                step(pending.pop(0)[1])
            elif inflight:
                it = inflight.pop(0)
                emit_exp(it)
                emit_pv(it)
            else:
                break
        c.barrier()
        c.release(kb1s + [b for (_, b) in kTs + qTs + Vhs] + [b for (_, b) in btst.items] + [b for (_, b) in ons.items])

    if "p6a" in phases:
        mlp_pass(3, 1, 0, None, None, scr["xbT"], pre=(scr["oT"], scr["x2T"], scr["x3T"], 2, wao_d))
    if "p6b" in phases:
        mlp_pass(3, 1, 1, scr["x3T"], scr["xbT"], outT, final=True)

    c.barrier()
    c.emit()
    return nc


def _t5_bucket_np(rel):
    import jax
    import jax.numpy as jnp
    nb = 16
    max_exact = 8
    with jax.default_device(jax.devices("cpu")[0]):
        rel = jnp.asarray(rel, dtype=jnp.int32)
        ret = jnp.where(rel > 0, nb, 0).astype(jnp.int32)
        n = jnp.abs(rel)
        nf = jnp.maximum(n, 1).astype(jnp.float32)
        large = max_exact + (jnp.log(nf / max_exact) / math.log(128 / max_exact) * (nb - max_exact)).astype(jnp.int32)
        large = jnp.minimum(large, nb - 1)
        return np.asarray(ret + jnp.where(n < max_exact, n, large))


def _fm(v, ncol):
    return np.ascontiguousarray(np.asarray(v, np.float32).reshape(ncol, 128).T)


def prep_shared(inp):
    g = lambda k: np.asarray(inp[k], np.float32)
    sh = {}
    sh["modw0"] = np.ascontiguousarray(g("mix_mod_w")[0])
    sh["modw1"] = np.ascontiguousarray(g("mlp_mod_w")[0])
    sh["modw2"] = np.ascontiguousarray(g("mix_mod_w")[1])
    sh["modw3"] = np.ascontiguousarray(g("mlp_mod_w")[1])
    sh["w_in"] = np.ascontiguousarray(g("ab_w_in")[0])
    sh["w_out"] = np.ascontiguousarray(g("ab_w_out")[0])
    sh["wqkv"] = np.ascontiguousarray(g("attn_w_qkv")[0])
    sh["wao"] = np.ascontiguousarray(g("attn_w_out")[0])
    for i in range(2):
        sh["w1_%d" % i] = np.ascontiguousarray(g("mlp_w1")[i])
        sh["w2_%d" % i] = np.ascontiguousarray(g("mlp_w2")[i])
    sh["ident"] = np.eye(128, dtype=np.float32)
    sh["sgwT"] = np.ascontiguousarray(np.transpose(g("sg_w")[0], (2, 0, 1)).reshape(128, 512))
    rel = np.arange(-(S - 1), S, dtype=np.int32)
    bucket = _t5_bucket_np(rel)
    tab = g("rel_bias_table")
    assert len(set(bucket[: S - 1 - 128].tolist())) == 1 and len(set(bucket[S - 1 + 129:].tolist())) == 1
    bt = np.empty((H, NDELTA, 128, 512), np.float32)
    ii = np.arange(128)[:, None]
    jj = np.arange(512)[None, :]
    for di in range(NDELTA):
        relm = 128 * (di - 1) + ii - jj
        bk = bucket[relm + S - 1]
        bt[:, di] = np.transpose(tab[bk], (2, 0, 1))
    sh["bt"] = bt
    cb = np.empty((H, 2), np.float32)
    cb[:, 0] = tab[bucket[0]]
    cb[:, 1] = tab[bucket[-1]]
    pvs = np.zeros((128, NPV), np.float32)

    def put(name, v, ncol):
        pvs[:, PV[name]:PV[name] + ncol] = _fm(v, ncol)
    put("mixg0", g("mix_norm_g")[0], 8)
    put("mlpg0", g("mlp_norm_g")[0], 8)
    put("mixg1", g("mix_norm_g")[1], 8)
    put("mlpg1", g("mlp_norm_g")[1], 8)
    put("fing", g("final_norm_g"), 8)
    put("modb0", g("mix_mod_b")[0], 24)
    put("modb1", g("mlp_mod_b")[0], 24)
    put("modb2", g("mix_mod_b")[1], 24)
    put("modb3", g("mlp_mod_b")[1], 24)
    put("convb", g("conv_b")[0], 4)
    put("clng", g("conv_ln_g")[0], 4)
    put("clnb", g("conv_ln_b")[0], 4)
    put("sgg", g("sg_ln_g")[0], 4)
    put("sublng", g("subln_g")[0], 1)
    cw = g("conv_w")[0]
    pvs[:, PV["convw"]:PV["convw"] + 124] = np.transpose(cw.reshape(31, 4, 128), (2, 1, 0)).reshape(128, 124)
    bcv = np.zeros((128, NBC), np.float32)
    bcv[:, 0:64] = g("lam_q1")[0][None]
    bcv[:, 64:128] = g("lam_k1")[0][None]
    bcv[:, 128:192] = g("lam_q2")[0][None]
    bcv[:, 192:256] = g("lam_k2")[0][None]
    bcv[:, BC["sgbb"]:BC["sgbb"] + 512] = g("sg_b")[0].reshape(1, 512)
    bcv[:, BC["bmat"]:BC["bmat"] + 512] = g("sg_ln_b")[0].reshape(1, 512)
    bcv[:, BC["cb"]:BC["cb"] + 16] = cb.reshape(1, 16)
    sh["bc"] = bcv
    return sh, pvs


def core_inputs(inp, b, sh, pvs):
    m = dict(sh)
    pvb = pvs.copy()
    pvb[:, PV["c"]:PV["c"] + 8] = _fm(np.asarray(inp["c"], np.float32)[b], 8)
    m["pv"] = pvb
    m["xT"] = np.ascontiguousarray(np.asarray(inp["x"], np.float32)[b].T)
    return m


_NC_CACHE = {}


def kernel(**inputs):
    sh, pvs = prep_shared(inputs)
    B = np.asarray(inputs["x"]).shape[0]
    if "full" not in _NC_CACHE:
        _NC_CACHE["full"] = build()
    nc = _NC_CACHE["full"]
    in_maps = [core_inputs(inputs, b, sh, pvs) for b in range(B)]
    res = run_bass_kernel_spmd(nc, in_maps, core_ids=list(range(B)))
    out = np.empty((B, S, D), np.float32)
    for b in range(B):
        out[b] = res.results[b]["outT"].T
    return out
```

```python
import math
import numpy as np
import concourse.bass as bass
import concourse.mybir as mybir
from concourse.bass_utils import run_bass_kernel_spmd

F32 = mybir.dt.float32
BF16 = mybir.dt.bfloat16
AF = mybir.ActivationFunctionType
ALU = mybir.AluOpType
AX = mybir.AxisListType

S = 8192
D = 1024
KC = 8
NT = 512
NTILES = S // NT
H = 8
DFF = 4096
RMS_EPS = 1e-6
LN_EPS = 1e-5
LAMBDA_INIT = 0.8 - 0.6 * math.exp(-0.3 * 1)
NDELTA = 6
STORE_Q = "pool"

PV = {}
_o = 0
for _n, _w in (("c", 8), ("mixg0", 8), ("mlpg0", 8), ("mixg1", 8), ("mlpg1", 8), ("fing", 8),
               ("modb0", 24), ("modb1", 24), ("modb2", 24), ("modb3", 24),
               ("convb", 4), ("clng", 4), ("clnb", 4), ("sgg", 4), ("sublng", 1), ("convw", 124)):
    PV[_n] = _o
    _o += _w
NPV = _o
BC = {"lam": 0, "sgbb": 256, "bmat": 768, "cb": 1280}
NBC = 1280 + 16


class Sem:
    __slots__ = ("h", "count")

    def __init__(self, h):
        self.h = h
        self.count = 0


class Buf:
    __slots__ = ("name", "w", "r", "sem")

    def __init__(self, name=""):
        self.name = name
        self.w = None
        self.r = []
        self.sem = None


class _FirstRec:
    def __init__(self, e):
        self.e = e
        self.first = None

    def __getattr__(self, name):
        f = getattr(self.e, name)

        def w(*a, **k):
            r = f(*a, **k)
            if self.first is None:
                self.first = r
            return r
        return w


class Ctx:
    def __init__(self, nc):
        self.nc = nc
        self.engs = {"pe": nc.tensor, "act": nc.scalar, "dve": nc.vector,
                     "pool": nc.gpsimd, "sp": nc.sync}
        self.prog = {e: [] for e in self.engs}
        self.esem = {e: Sem(nc.alloc_semaphore("es_" + e)) for e in ("pe", "act", "dve", "pool")}
        self.seen = {e: {} for e in self.engs}
        self.free_sems = []
        self.all_dsems = []

    def dsem(self):
        if self.free_sems:
            return self.free_sems.pop()
        s = Sem(self.nc.alloc_semaphore("ds%d" % len(self.all_dsems)))
        self.all_dsems.append(s)
        return s

    def release(self, bufs):
        for b in bufs:
            if b.sem is not None:
                self.free_sems.append(b.sem)
                b.sem = None

    def _deps(self, eng, reads, writes):
        deps = {}
        for b in reads:
            if b.w is not None:
                s, v = b.w
                if deps.get(s, 0) < v:
                    deps[s] = v
        for b in writes:
            if b.w is not None:
                s, v = b.w
                if deps.get(s, 0) < v:
                    deps[s] = v
            for (s, v) in b.r:
                if deps.get(s, 0) < v:
                    deps[s] = v
        waits = []
        seen = self.seen[eng]
        own = self.esem.get(eng)
        for s, v in deps.items():
            if seen.get(s, 0) >= v:
                continue
            if s is own and (eng == "pe" or s.count - v >= 4):
                continue
            seen[s] = v
            waits.append((s.h, v))
        return waits

    def op(self, eng, fn, reads=(), writes=()):
        waits = self._deps(eng, reads, writes)
        s = self.esem[eng]
        s.count += 1
        done = (s, s.count)
        for b in reads:
            b.r.append(done)
        for b in writes:
            b.w = done
            b.r = []
        self.prog[eng].append((waits, fn, s.h, 1))

    def dma(self, q, out_ap, in_ap, reads, writes, sembuf):
        waits = self._deps(q, reads, writes)
        if sembuf.sem is None:
            sembuf.sem = self.dsem()
        s = sembuf.sem
        s.count += 16
        done = (s, s.count)
        for b in reads:
            b.r.append(done)
        for b in writes:
            b.w = done
            b.r = []

        def fn(e, out_ap=out_ap, in_ap=in_ap):
            return e.dma_start(out=out_ap, in_=in_ap)
        self.prog[q].append((waits, fn, s.h, 16))

    def barrier(self):
        targets = [(s, s.count) for s in self.esem.values() if s.count > 0]
        targets += [(s, s.count) for s in self.all_dsems if s.count > 0]
        for e in self.engs:
            seen = self.seen[e]
            waits = []
            for s, v in targets:
                if seen.get(s, 0) >= v:
                    continue
                seen[s] = v
                waits.append((s.h, v))
            if waits:
                self.prog[e].append((waits, None, None, 0))

    def emit(self):
        nc = self.nc
        with nc.Block() as block:
            def mk(name):
                def body(e):
                    for waits, fn, sh, inc in self.prog[name]:
                        if fn is None:
                            for (h, v) in waits:
                                e.wait_ge(h, v)
                            continue
                        for (h, v) in waits:
                            e.wait_ge(h, v)
                        r = fn(e)
                        last = r[1] if isinstance(r, tuple) else r
                        last.then_inc(sh, inc)
                return body
            block.sync(mk("sp"))
            block.scalar(mk("act"))
            block.vector(mk("dve"))
            block.gpsimd(mk("pool"))
            block.tensor(mk("pe"))


class Arena:
    def __init__(self, nc, nbytes):
        self.n = nbytes // 2
        self.t = nc.alloc_sbuf_tensor("arena", [128, self.n], BF16)
        self.off = 0

    def reset(self):
        self.off = 0

    def alloc(self, shape, dt):
        n = int(np.prod(shape))
        units = n * (2 if dt == F32 else 1)
        self.off = (self.off + 15) // 16 * 16
        assert self.off + units <= self.n, ("arena overflow", self.off, units, self.n)
        ap = self.t[:, self.off:self.off + units]
        self.off += units
        if dt == F32:
            ap = ap.bitcast(F32)
        if len(shape) == 2:
            ap = ap.rearrange("p (a b) -> p a b", a=shape[0])
        elif len(shape) == 3:
            ap = ap.rearrange("p (a b c) -> p a b c", a=shape[0], b=shape[1])
        return ap, Buf()


class Rot:
    def __init__(self, items):
        self.items = items
        self.i = 0

    def next(self):
        it = self.items[self.i % len(self.items)]
        self.i += 1
        return it


ALL_PHASES = ("p1", "p2a", "p2b", "p3", "p4", "p6a", "p6b")
SCRATCH = {
    "mixT": ([D, S], BF16, "p1"),
    "x1T": ([D, S], F32, "p2a"),
    "xaT": ([D, S], F32, "p2a"),
    "x2T": ([D, S], F32, "p2b"),
    "qT": ([D, S], BF16, "p3"),
    "kT": ([D, S], BF16, "p3"),
    "Vs": ([H, 128, S // 128, 128], BF16, "p3"),
    "oT": ([D, S], BF16, "p4"),
    "x3T": ([D, S], F32, "p6a"),
    "xbT": ([D, S], F32, "p6a"),
}
CONSUMES = {
    "p1": [], "p2a": ["mixT"], "p2b": ["x1T", "xaT"], "p3": ["x2T"], "p4": ["qT", "kT", "Vs"],
    "p6a": ["oT", "x2T"], "p6b": ["x3T", "xbT"],
}


def build(phases=ALL_PHASES, ntiles=NTILES, nheads=H, debug_out=False):
    nc = bass.Bass("TRN2", target_bir_lowering=False)
    phases = tuple(phases)

    def din(name, shape, dt=F32):
        return nc.dram_tensor(name, list(shape), dt, kind="ExternalInput").ap()

    xT = din("xT", [D, S])
    pv_d = din("pv", [128, NPV])
    bc_d = din("bc", [128, NBC])
    sgwT_d = din("sgwT", [128, 512])
    ident_d = din("ident", [128, 128])
    modw_d = [din("modw%d" % i, [D, 3 * D]) for i in range(4)]
    w_in_d = din("w_in", [D, 2048])
    w_out_d = din("w_out", [D, D])
    wqkv_d = din("wqkv", [D, 3 * D])
    wao_d = din("wao", [D, D])
    w1_d = [din("w1_%d" % i, [D, DFF]) for i in range(2)]
    w2_d = [din("w2_%d" % i, [DFF, D]) for i in range(2)]
    bt_d = din("bt", [H, NDELTA, 128, 512])
    scr = {}
    for name, (shape, dt, prod) in SCRATCH.items():
        produced = prod in phases
        consumed = any(name in CONSUMES[p] for p in phases)
        if produced:
            later = any(name in CONSUMES[p] for p in ALL_PHASES if p not in phases)
            kind = "ExternalOutput" if (later or debug_out) else "Internal"
        elif consumed:
            kind = "ExternalInput"
        else:
            continue
        scr[name] = nc.dram_tensor(name, list(shape), dt, kind=kind).ap()
    outT = None
    if "p6b" in phases:
        outT = nc.dram_tensor("outT", [D, S], F32, kind="ExternalOutput").ap()

    c = Ctx(nc)
    dbg_bufs = []

    def dbg_dump(name, ap, buf, dt=F32):
        if not debug_out:
            return
        shp = [int(x) for x in ap.shape]
        d_ = nc.dram_tensor("dbg_" + name, shp, dt, kind="ExternalOutput").ap()
        b_ = Buf()
        c.dma("sp", d_, ap, [buf], [], b_)
        dbg_bufs.append(b_)
    pv = nc.alloc_sbuf_tensor("pv_sb", [128, NPV], F32)
    bcs = nc.alloc_sbuf_tensor("bc_sb", [128, NBC], F32)
    modv = nc.alloc_sbuf_tensor("modv", [128, 96], F32)
    coefA = nc.alloc_sbuf_tensor("coefA", [128, 32], F32)
    cact = nc.alloc_sbuf_tensor("cact", [128, 8], F32)
    ones_bf = nc.alloc_sbuf_tensor("ones_bf", [128, 128], BF16)
    ones_f = nc.alloc_sbuf_tensor("ones_f", [128, 128], F32)
    b_onesf = Buf()
    ident_f = nc.alloc_sbuf_tensor("ident_f", [128, 128], F32)
    ident_bf = nc.alloc_sbuf_tensor("ident_bf", [128, 128], BF16)
    small = nc.alloc_sbuf_tensor("small", [128, 16], F32)
    lamp = nc.alloc_sbuf_tensor("lamp", [128, 128], F32)
    b_pv, b_bc, b_modv, b_coef, b_cact, b_ones, b_identf, b_identb, b_small, b_lamp = [Buf() for _ in range(10)]
    arena = Arena(nc, 199 * 1024)
    pp = [nc.alloc_psum_tensor("pp%d" % i, [128, 2, 512], F32) for i in range(4)]
    bank = [pp[i // 2][:, i % 2, :] for i in range(8)]
    bankb = [Buf("bank%d" % i) for i in range(8)]

    GATE = lambda s, m: modv[:, s * 24 + 16 + m: s * 24 + 17 + m]
    SHIFT = lambda s, m: modv[:, s * 24 + m: s * 24 + m + 1]
    COEF = lambda s, m: coefA[:, s * 8 + m: s * 8 + m + 1]
    NORMG = {0: "mixg0", 1: "mlpg0", 2: "mixg1", 3: "mlpg1"}

    c.dma("sp", pv[:], pv_d, [], [b_pv], b_pv)
    c.dma("sp", bcs[:], bc_d, [], [b_bc], b_bc)
    c.dma("sp", ident_f[:], ident_d, [], [b_identf], b_identf)
    c.op("pool", lambda e: e.memset(ones_bf[:], 1.0), [], [b_ones])
    c.op("pool", lambda e: e.memset(ones_f[:], 1.0), [], [b_onesf])
    c.op("pool", lambda e: e.tensor_copy(out=ident_bf[:], in_=ident_f[:]), [b_identf], [b_identb])
    c.op("act", lambda e: e.activation(out=cact[:], in_=pv[:, PV["c"]:PV["c"] + 8], func=AF.Silu), [b_pv], [b_cact])
    L = BC["lam"]
    c.op("dve", lambda e: e.tensor_tensor(out=lamp[:, 0:64], in0=bcs[:, L:L + 64], in1=bcs[:, L + 64:L + 128], op=ALU.mult), [b_bc], [b_lamp])
    c.op("dve", lambda e: e.tensor_tensor(out=lamp[:, 64:128], in0=bcs[:, L + 128:L + 192], in1=bcs[:, L + 192:L + 256], op=ALU.mult), [b_bc], [b_lamp])
    c.op("dve", lambda e: e.tensor_reduce(out=small[:, 0:2], in_=lamp[:].rearrange("p (a b) -> p a b", a=2), axis=AX.X, op=ALU.add), [b_lamp], [b_small])
    c.op("act", lambda e: e.activation(out=small[:, 2:4], in_=small[:, 0:2], func=AF.Exp), [b_small], [b_small])
    c.op("dve", lambda e: e.tensor_tensor(out=small[:, 4:5], in0=small[:, 3:4], in1=small[:, 2:3], op=ALU.subtract), [b_small], [b_small])
    c.op("dve", lambda e: e.tensor_scalar(out=small[:, 4:5], in0=small[:, 4:5], scalar1=-LAMBDA_INIT, scalar2=None, op0=ALU.add), [b_small], [b_small])
    c.op("dve", lambda e: e.tensor_scalar(out=small[:, 5:6], in0=pv[:, PV["sublng"]:PV["sublng"] + 1], scalar1=1.0 - LAMBDA_INIT, scalar2=None, op0=ALU.mult), [b_pv, b_small], [b_small])
    NEGLAM = small[:, 4:5]
    SUBLNG = small[:, 5:6]

    arena.reset()
    wst = [arena.alloc((8, 1536), F32) for _ in range(2)]
    mps = pp[0][:, 0, 0:96]
    for s in range(4):
        for half in range(2):
            wb, wbuf = wst[(s * 2 + half) % 2]
            src = modw_d[s][:, half * 1536:(half + 1) * 1536].rearrange("(k p) n -> p k n", p=128)
            c.dma("sp", wb, src, [], [wbuf], wbuf)

            def f(e, wb=wb, s=s, half=half):
                ins = None
                for j in range(12):
                    col = s * 24 + half * 12 + j
                    for k in range(KC):
                        ins = e.matmul(mps[:, col:col + 1], wb[:, k, j * 128:(j + 1) * 128], cact[:, k:k + 1],
                                       start=(k == 0), stop=(k == KC - 1))
                return ins
            c.op("pe", f, [wbuf, b_cact], [bankb[0]])
    for s in range(4):
        o = PV["modb%d" % s]
        c.op("dve", lambda e, s=s, o=o: e.tensor_tensor(out=modv[:, s * 24:(s + 1) * 24], in0=mps[:, s * 24:(s + 1) * 24],
                                                       in1=pv[:, o:o + 24], op=ALU.add), [bankb[0], b_pv], [b_modv])
        g = PV[NORMG[s]]
        c.op("dve", lambda e, s=s, g=g: e.scalar_tensor_tensor(out=coefA[:, s * 8:(s + 1) * 8], in0=modv[:, s * 24 + 8:s * 24 + 16],
                                                              scalar=1.0, in1=pv[:, g:g + 8], op0=ALU.add, op1=ALU.mult),
             [b_modv, b_pv], [b_coef])
    c.barrier()
    c.release([wst[0][1], wst[1][1]])

    stage_rot_engines = ("pool", "dve")

    def load_w_bf16(dst, src, kc, ncols, stages, cnt=[0]):
        step = 2048
        for k in range(kc):
            for n0 in range(0, ncols, step):
                w = min(step, ncols - n0)
                st, sb = stages[cnt[0] % len(stages)]
                eng = stage_rot_engines[cnt[0] % 2]
                cnt[0] += 1
                c.dma("sp", st[:, 0:w], src[k * 128:(k + 1) * 128, n0:n0 + w], [], [sb], sb)
                tmpb = Buf()
                c.op(eng, lambda e, st=st, k=k, n0=n0, w=w: e.tensor_copy(out=dst[:, k, n0:n0 + w], in_=st[:, 0:w]), [sb], [tmpb])

    def rstd_from_psum(ps_ap, rstd_ap, n, scale, eps, reads, writes):
        c.op("act", lambda e: e.activation(out=rstd_ap, in_=ps_ap, func=AF.Ln, scale=scale, bias=eps), reads, writes)
        c.op("act", lambda e: e.activation(out=rstd_ap, in_=rstd_ap, func=AF.Exp, scale=-0.5), writes, writes)

    def norm_h(x_ap, xb, W, s, sq, sqb, rstd, rstdb, tts, h_ap, hb, psrot, mid=None):
        c.op("act", lambda e: e.activation(out=sq[:, :, 0:W], in_=x_ap, func=AF.Square), [xb], [sqb])
        if mid is not None:
            mid()
        groups = [(0, min(512, W))] + ([(512, W)] if W > 512 else [])
        for (c0, c1) in groups:
            pi = psrot.next()
            n = c1 - c0

            def f(e, pi=pi, c0=c0, c1=c1, n=n):
                ins = None
                for k in range(KC):
                    ins = e.matmul(bank[pi][:, 0:n], ones_bf[:], sq[:, k, c0:c1], start=(k == 0), stop=(k == KC - 1))
                return ins
            c.op("pe", f, [sqb, b_ones], [bankb[pi]])
            rstd_from_psum(bank[pi][:, 0:n], rstd[:, c0:c1], n, 1.0 / D, RMS_EPS, [bankb[pi]], [rstdb])
        for k in range(KC):
            t, tb = tts.next()
            c.op("dve", lambda e, t=t, k=k: e.tensor_tensor(out=t[:, 0:W], in0=x_ap[:, k, :], in1=rstd[:, 0:W], op=ALU.mult), [xb, rstdb], [tb])
            c.op("act", lambda e, t=t, k=k: e.activation(out=h_ap[:, k, 0:W], in_=t[:, 0:W], func=AF.Identity,
                                                        scale=COEF(s, k), bias=SHIFT(s, k)), [tb, b_coef, b_modv], [hb])

    def mm_group(pi, n, lhs_list, rhs_list, reads):
        def f(e):
            ins = first = None
            L_ = len(lhs_list)
            for i in range(L_):
                ins = e.matmul(bank[pi][:, 0:n], lhs_list[i], rhs_list[i], start=(i == 0), stop=(i == L_ - 1))
                if first is None:
                    first = ins
            return first, ins
        c.op("pe", f, reads, [bankb[pi]])

    if "p1" in phases:
        arena.reset()
        w_in_sb, _ = arena.alloc((8, 2048), BF16)
        dg, b_dg = arena.alloc((4, 31, 128), BF16)
        sgw_f, b_sgwf = arena.alloc((4, 128), F32)
        sgw_bf, b_sgwb = arena.alloc((4, 128), BF16)
        Rt, b_R = arena.alloc((4, 128), F32)
        mark = arena.off
        stages = [arena.alloc((2048,), F32) for _ in range(3)]
        load_w_bf16(w_in_sb, w_in_d, 8, 2048, stages)
        c.dma("sp", sgw_f, sgwT_d.rearrange("p (a b) -> p a b", a=4), [], [b_sgwf], b_sgwf)
        c.op("pool", lambda e: e.tensor_copy(out=sgw_bf, in_=sgw_f), [b_sgwf], [b_sgwb])
        for cc in range(4):
            o = PV["convw"] + cc * 31
            c.op("dve", lambda e, cc=cc, o=o: e.tensor_tensor(
                out=dg[:, cc, :, :], in0=ident_f[:].unsqueeze(1).broadcast_to([128, 31, 128]),
                in1=pv[:, o:o + 31].unsqueeze(2).broadcast_to([128, 31, 128]), op=ALU.mult), [b_identf, b_pv], [b_dg])
        bm = BC["bmat"]

        def fR(e):
            ins = None
            for g in range(4):
                ins = e.matmul(bank[1][:, g * 128:(g + 1) * 128], bcs[:, bm + g * 128:bm + (g + 1) * 128], sgw_f[:, g, :],
                               start=True, stop=True)
            return ins
        c.op("pe", fR, [b_bc, b_sgwf], [bankb[1]])
        c.op("dve", lambda e: e.tensor_tensor(out=Rt.rearrange("p a b -> p (a b)"), in0=bank[1], in1=bcs[:, BC["sgbb"]:BC["sgbb"] + 512],
                                              op=ALU.add), [bankb[1], b_bc], [b_R])
        c.barrier()
        c.release([st[1] for st in stages])
        arena.off = mark
        WX = 544
        xes = Rot([arena.alloc((8, WX), F32) for _ in range(2)])
        sq, sqb = arena.alloc((8, WX), BF16)
        rstd, rstdb = arena.alloc((WX,), F32)
        tts = Rot([arena.alloc((WX,), F32) for _ in range(3)])
        hs = Rot([arena.alloc((8, WX), BF16) for _ in range(1)])
        a_t, a_b = arena.alloc((4, WX), BF16)
        sgts = Rot([arena.alloc((512,), F32) for _ in range(2)])
        sgh, sghb = arena.alloc((4, 32), F32)
        yb, ybb = arena.alloc((4, 512), BF16)
        ysq, ysqb = arena.alloc((4, 512), BF16)
        mean_t, meanb = arena.alloc((512,), F32)
        msq_t, msqb = arena.alloc((512,), F32)
        rs2, rs2b = arena.alloc((512,), F32)
        dts = Rot([arena.alloc((512,), F32) for _ in range(2)])
        mixeds = Rot([arena.alloc((8, 512), BF16) for _ in range(2)])
        ug, ugb = arena.alloc((4, 512), BF16)
        gvs = Rot([arena.alloc((512,), F32) for _ in range(2)])
        st6, st6b = arena.alloc((8,), F32)
        mv, mvb = arena.alloc((4,), F32)
        vn, vnb = arena.alloc((4, 512), BF16)
        sgtmp = Rot([arena.alloc((512,), F32) for _ in range(2)])
        for (ap_, b_) in xes.items:
            c.op("pool", lambda e, ap_=ap_: e.memset(ap_, 0.0), [], [b_])
        psrot = Rot([0, 1, 2, 3, 4, 5, 6, 7])
        for i in range(ntiles):
            t0 = i * NT
            mixed, mixb = mixeds.next()
            lo, hi = t0 - 16, t0 + NT + 16
            slo, shi = max(lo, 0), min(hi, S)
            xe, xeb = xes.next()
            c.dma("sp", xe[:, :, slo - lo:shi - lo], xT[:, slo:shi].rearrange("(k p) t -> p k t", p=128), [], [xeb], xeb)
            h_ap, hb = hs.next()
            norm_h(xe, xeb, WX, 0, sq, sqb, rstd, rstdb, tts, h_ap, hb, psrot)
            ph = psrot.next()
            for j in range(4):
                pa, pg = psrot.next(), psrot.next()
                mm_group(pa, 512, [w_in_sb[:, k, j * 128:(j + 1) * 128] for k in range(KC)], [h_ap[:, k, 0:512] for k in range(KC)], [hb])
                mm_group(pg, 512, [w_in_sb[:, k, 512 + j * 128:512 + (j + 1) * 128] for k in range(KC)], [h_ap[:, k, 0:512] for k in range(KC)], [hb])
                sgt, sgtb = sgts.next()
                c.op("act", lambda e, pg=pg, sgt=sgt: e.activation(out=sgt, in_=bank[pg], func=AF.Sigmoid), [bankb[pg]], [sgtb])
                c.op("dve", lambda e, pa=pa, sgt=sgt, j=j: e.tensor_tensor(out=a_t[:, j, 0:512], in0=bank[pa], in1=sgt, op=ALU.mult),
                     [bankb[pa], sgtb], [a_b])

            def fh(e, ph=ph, h_ap=h_ap):
                ins = None
                for j in range(4):
                    for g2 in range(2):
                        for k in range(KC):
                            ins = e.matmul(bank[ph][:, j * 64 + g2 * 32:j * 64 + g2 * 32 + 32],
                                           w_in_sb[:, k, g2 * 512 + j * 128:g2 * 512 + (j + 1) * 128], h_ap[:, k, 512:544],
                                           start=(k == 0), stop=(k == KC - 1))
                return ins
            c.op("pe", fh, [hb], [bankb[ph]])
            phv = bank[ph][:, 0:256].rearrange("p (j g n) -> p j g n", j=4, g=2)
            c.op("act", lambda e, phv=phv: e.activation(out=sgh, in_=phv[:, :, 1, :], func=AF.Sigmoid), [bankb[ph]], [sghb])
            c.op("dve", lambda e, phv=phv: e.tensor_tensor(out=a_t[:, :, 512:544], in0=phv[:, :, 0, :], in1=sgh, op=ALU.mult),
                 [bankb[ph], sghb], [a_b])
            if lo < 0:
                c.op("pool", lambda e: e.memset(a_t[:, :, 0:16], 0.0), [], [a_b])
            if hi > S:
                c.op("pool", lambda e: e.memset(a_t[:, :, 528:544], 0.0), [], [a_b])
            for cc in range(4):
                pc = psrot.next()
                mm_group(pc, 512, [dg[:, cc, w, :] for w in range(31)], [a_t[:, cc, w + 1:w + 513] for w in range(31)], [a_b, b_dg])
                cb_ap = pv[:, PV["convb"] + cc:PV["convb"] + cc + 1]
                c.op("act", lambda e, pc=pc, cc=cc, cb_ap=cb_ap: e.activation(out=yb[:, cc, :], in_=bank[pc], func=AF.Identity, bias=cb_ap),
                     [bankb[pc], b_pv], [ybb])
                c.op("act", lambda e, pc=pc, cc=cc, cb_ap=cb_ap: e.activation(out=ysq[:, cc, :], in_=bank[pc], func=AF.Square, bias=cb_ap),
                     [bankb[pc], b_pv], [ysqb])
            p1_, p2_ = psrot.next(), psrot.next()
            mm_group(p1_, 512, [ones_bf[:]] * 4, [yb[:, cc, :] for cc in range(4)], [ybb, b_ones])
            mm_group(p2_, 512, [ones_bf[:]] * 4, [ysq[:, cc, :] for cc in range(4)], [ysqb, b_ones])
            c.op("dve", lambda e, p1_=p1_: e.tensor_scalar(out=mean_t, in0=bank[p1_], scalar1=1.0 / 512, scalar2=None, op0=ALU.mult), [bankb[p1_]], [meanb])
            c.op("dve", lambda e: e.tensor_tensor(out=msq_t, in0=mean_t, in1=mean_t, op=ALU.mult), [meanb], [msqb])
            c.op("dve", lambda e, p2_=p2_: e.scalar_tensor_tensor(out=msq_t, in0=bank[p2_], scalar=1.0 / 512, in1=msq_t, op0=ALU.mult, op1=ALU.subtract),
                 [bankb[p2_], msqb], [msqb])
            c.op("act", lambda e: e.activation(out=rs2, in_=msq_t, func=AF.Ln, bias=LN_EPS), [msqb], [rs2b])
            c.op("act", lambda e: e.activation(out=rs2, in_=rs2, func=AF.Exp, scale=-0.5), [rs2b], [rs2b])
            for cc in range(4):
                dt_, dtb = dts.next()
                c.op("dve", lambda e, dt_=dt_, cc=cc: e.tensor_tensor(out=dt_, in0=yb[:, cc, :], in1=mean_t, op=ALU.subtract), [ybb, meanb], [dtb])
                c.op("dve", lambda e, dt_=dt_: e.tensor_tensor(out=dt_, in0=dt_, in1=rs2, op=ALU.mult), [dtb, rs2b], [dtb])
                g_ap = pv[:, PV["clng"] + cc:PV["clng"] + cc + 1]
                b_ap = pv[:, PV["clnb"] + cc:PV["clnb"] + cc + 1]
                c.op("act", lambda e, dt_=dt_, cc=cc, g_ap=g_ap, b_ap=b_ap, mixed=mixed: e.activation(out=mixed[:, cc, :], in_=dt_, func=AF.Silu, scale=g_ap, bias=b_ap),
                     [dtb, b_pv], [mixb])
            for g in range(4):
                pu = psrot.next()
                mm_group(pu, 512, [w_in_sb[:, k, 1024 + g * 128:1024 + (g + 1) * 128] for k in range(KC)], [h_ap[:, k, 16:528] for k in range(KC)], [hb])
                c.op("act", lambda e, pu=pu, g=g: e.activation(out=ug[:, g, :], in_=bank[pu], func=AF.Gelu), [bankb[pu]], [ugb])
            for n in range(4):
                pvb = psrot.next()
                mm_group(pvb, 512, [h_ap[:, k, 16 + n * 128:16 + (n + 1) * 128] for k in range(KC)], [w_in_sb[:, k, 1536:2048] for k in range(KC)], [hb])
                gv, gvb = gvs.next()
                c.op("act", lambda e, pvb=pvb, gv=gv: e.activation(out=gv, in_=bank[pvb], func=AF.Gelu), [bankb[pvb]], [gvb])
                c.op("dve", lambda e, gv=gv: e.bn_stats(out=st6[:, 0:6], in_=gv), [gvb], [st6b])
                c.op("dve", lambda e: e.bn_aggr(out=mv[:, 0:2], in_=st6[:, 0:6]), [st6b], [mvb])
                c.op("act", lambda e: e.activation(out=mv[:, 2:3], in_=mv[:, 1:2], func=AF.Ln, bias=LN_EPS), [mvb], [mvb])
                c.op("act", lambda e: e.activation(out=mv[:, 2:3], in_=mv[:, 2:3], func=AF.Exp, scale=-0.5), [mvb], [mvb])
                c.op("dve", lambda e, gv=gv, n=n: e.tensor_scalar(out=vn[:, n, :], in0=gv, scalar1=mv[:, 0:1], scalar2=mv[:, 2:3],
                                                               op0=ALU.subtract, op1=ALU.mult), [gvb, mvb], [vnb])
            for g in range(4):
                pm = psrot.next()

                def fsp(e, pm=pm, g=g):
                    ins = None
                    for n in range(4):
                        ins = e.matmul(bank[pm][:, n * 128:(n + 1) * 128], vn[:, n, g * 128:(g + 1) * 128], sgw_bf[:, g, :], start=True, stop=True)
                    return ins
                c.op("pe", fsp, [vnb, b_sgwb], [bankb[pm]])
                tm, tmb = sgtmp.next()
                gg = pv[:, PV["sgg"] + g:PV["sgg"] + g + 1]
                c.op("dve", lambda e, pm=pm, tm=tm, g=g, gg=gg: e.scalar_tensor_tensor(
                    out=tm.rearrange("p (a b) -> p a b", a=4), in0=bank[pm].rearrange("p (a b) -> p a b", a=4), scalar=gg,
                    in1=Rt[:, g, :].unsqueeze(1).broadcast_to([128, 4, 128]), op0=ALU.mult, op1=ALU.add), [bankb[pm], b_R, b_pv], [tmb])
                c.op("pool", lambda e, tm=tm, g=g, mixed=mixed: e.tensor_tensor(out=mixed[:, 4 + g, :], in0=tm, in1=ug[:, g, :], op=ALU.mult), [tmb, ugb], [mixb])
            if i == 0:
                dbg_dump("a0", a_t, a_b, BF16)
                dbg_dump("yb0", yb, ybb, BF16)
                dbg_dump("mean0", mean_t, meanb)
                dbg_dump("var0", msq_t, msqb)
                dbg_dump("rs20", rs2, rs2b)
                dbg_dump("h0", h_ap, hb, BF16)
                dbg_dump("rstd0", rstd, rstdb)
            if i == 1:
                dbg_dump("u1", ug, ugb, BF16)
                dbg_dump("vn1", vn, vnb, BF16)
                dbg_dump("R", Rt, b_R)
                dbg_dump("tm1", tm, tmb)
            c.dma(STORE_Q, scr["mixT"][:, t0:t0 + NT].rearrange("(k p) t -> p k t", p=128), mixed, [mixb], [], mixb)
        c.barrier()
        c.release([b for (_, b) in xes.items] + [b for (_, b) in mixeds.items])

    def mlp_pass(s, layer, half, src_d, res_d, out_d, final=False, pre=None):
        arena.reset()
        w1h, _ = arena.alloc((8, 2048), BF16)
        w2h, _ = arena.alloc((16, 1024), BF16)
        if pre is not None:
            wpre_sb, _ = arena.alloc((8, 1024), BF16)
        mark = arena.off
        stages = [arena.alloc((2048,), F32) for _ in range(3)]
        load_w_bf16(w1h, w1_d[layer][:, half * 2048:(half + 1) * 2048], 8, 2048, stages)
        load_w_bf16(w2h, w2_d[layer][half * 2048:(half + 1) * 2048, :], 16, 1024, stages)
        if pre is not None:
            load_w_bf16(wpre_sb, pre[4], 8, 1024, stages)
        c.barrier()
        c.release([st[1] for st in stages])
        arena.off = mark
        xss = Rot([arena.alloc((8, 512), F32) for _ in range(2)])
        rch = Rot([arena.alloc((512,), F32) for _ in range(4)])
        sq, sqb = arena.alloc((8, 512), BF16)
        rstd, rstdb = arena.alloc((512,), F32)
        tts = Rot([arena.alloc((512,), F32) for _ in range(3)])
        hs = Rot([arena.alloc((8, 512), BF16) for _ in range(2)])
        rts = Rot([arena.alloc((512,), BF16) for _ in range(3)])
        hid, hidb = arena.alloc((16, 512), BF16)
        if final:
            xof, xofb = arena.alloc((8, 512), F32)
            sq2, sq2b = arena.alloc((8, 512), BF16)
            rstd2, rstd2b = arena.alloc((512,), F32)
        else:
            och = Rot([arena.alloc((512,), F32) for _ in range(4)])
        if pre is not None:
            ots = Rot([arena.alloc((8, 512), BF16) for _ in range(1)])
        psrot = Rot([0, 1, 2, 3, 4, 5, 6, 7])
        tr = lambda d_, t0: d_[:, t0:t0 + NT].rearrange("(k p) t -> p k t", p=128)
        ch = lambda d_, m, t0: d_[m * 128:(m + 1) * 128, t0:t0 + NT]
        def stage_a1(i):
            t0 = i * NT
            xs, xsb = xss.next()
            if pre is None:
                c.dma("sp", xs, tr(src_d, t0), [], [xsb], xsb)
            else:
                o_d, xres_d, xout_d, sa, _w = pre
                ot, otb = ots.next()
                c.dma("sp", ot, tr(o_d, t0), [], [otb], otb)
                for m in range(KC):
                    rc, rcb = rch.next()
                    c.dma("sp", rc, ch(xres_d, m, t0), [], [rcb], rcb)
                    po = psrot.next()
                    mm_group(po, 512, [wpre_sb[:, k, m * 128:(m + 1) * 128] for k in range(KC)], [ot[:, k, :] for k in range(KC)], [otb])
                    c.op("dve", lambda e, po=po, m=m, xs=xs, rc=rc: e.scalar_tensor_tensor(out=xs[:, m, :], in0=bank[po], scalar=GATE(sa, m), in1=rc,
                                                                                       op0=ALU.mult, op1=ALU.add), [bankb[po], rcb, b_modv], [xsb])
                c.dma(STORE_Q, tr(xout_d, t0), xs, [xsb], [], xsb)
            h_ap, hb = hs.next()
            return {"t0": t0, "xs": xs, "xsb": xsb, "h": h_ap, "hb": hb}

        def stage_a2(st, mid=None):
            norm_h(st["xs"], st["xsb"], 512, s, sq, sqb, rstd, rstdb, tts, st["h"], st["hb"], psrot, mid=mid)

        def stage_b(st):
            h_ap, hb = st["h"], st["hb"]
            for j in range(16):
                pj = psrot.next()
                mm_group(pj, 512, [w1h[:, k, j * 128:(j + 1) * 128] for k in range(KC)], [h_ap[:, k, :] for k in range(KC)], [hb])
                rt, rtb = rts.next()
                c.op("act", lambda e, pj=pj, rt=rt: e.activation(out=rt, in_=bank[pj], func=AF.Relu), [bankb[pj]], [rtb])
                c.op("dve", lambda e, pj=pj, rt=rt, j=j: e.scalar_tensor_tensor(out=hid[:, j, :], in0=bank[pj], scalar=0.0, in1=rt, op0=ALU.max, op1=ALU.mult),
                     [bankb[pj], rtb], [hidb])

        def stage_c(st, m0, m1):
            t0, xs, xsb = st["t0"], st["xs"], st["xsb"]
            for m in range(m0, m1):
                if res_d is not None:
                    rc, rcb = rch.next()
                    c.dma("sp", rc, ch(res_d, m, t0), [], [rcb], rcb)
                else:
                    rc, rcb = xs[:, m, :], xsb
                po = psrot.next()
                mm_group(po, 512, [w2h[:, j, m * 128:(m + 1) * 128] for j in range(16)], [hid[:, j, :] for j in range(16)], [hidb])
                if final:
                    oc, ocb = xof[:, m, :], xofb
                else:
                    oc, ocb = och.next()
                c.op("dve", lambda e, po=po, m=m, oc=oc, rc=rc: e.scalar_tensor_tensor(out=oc, in0=bank[po], scalar=GATE(s, m), in1=rc,
                                                                                   op0=ALU.mult, op1=ALU.add), [bankb[po], rcb, b_modv], [ocb])
                if not final:
                    c.dma(STORE_Q, ch(out_d, m, t0), oc, [ocb], [], ocb)

        def stage_fin(st):
            t0 = st["t0"]
            c.op("act", lambda e: e.activation(out=sq2, in_=xof, func=AF.Square), [xofb], [sq2b])
            pf = psrot.next()
            mm_group(pf, 512, [ones_bf[:]] * KC, [sq2[:, k, :] for k in range(KC)], [sq2b, b_ones])
            rstd_from_psum(bank[pf], rstd2, 512, 1.0 / D, RMS_EPS, [bankb[pf]], [rstd2b])
            for m in range(KC):
                fg = pv[:, PV["fing"] + m:PV["fing"] + m + 1]
                c.op("dve", lambda e, m=m, fg=fg: e.scalar_tensor_tensor(out=xof[:, m, :], in0=xof[:, m, :], scalar=fg, in1=rstd2, op0=ALU.mult, op1=ALU.mult),
                     [xofb, rstd2b, b_pv], [xofb])
            c.dma(STORE_Q, tr(out_d, t0), xof, [xofb], [], xofb)

        cur = stage_a1(0)
        stage_a2(cur)
        for i in range(ntiles):
            stage_b(cur)
            nxt = None
            if i + 1 < ntiles:
                nxt = stage_a1(i + 1)
                stage_a2(nxt, mid=lambda cur=cur: stage_c(cur, 0, 4))
            else:
                stage_c(cur, 0, 4)
            stage_c(cur, 4, KC)
            if final:
                stage_fin(cur)
            cur = nxt
        c.barrier()
        rel = [b for (_, b) in xss.items] + [b for (_, b) in rch.items]
        if final:
            rel.append(xofb)
        else:
            rel += [b for (_, b) in och.items]
        if pre is not None:
            rel += [b for (_, b) in ots.items]
        c.release(rel)

    if "p2a" in phases:
        mlp_pass(1, 0, 0, None, None, scr["xaT"], pre=(scr["mixT"], xT, scr["x1T"], 0, w_out_d))
    if "p2b" in phases:
        mlp_pass(1, 0, 1, scr["x1T"], scr["xaT"], scr["x2T"])

    if "p3" in phases:
        arena.reset()
        wq, _ = arena.alloc((8, 3072), BF16)
        stages = [arena.alloc((2048,), F32) for _ in range(3)]
        load_w_bf16(wq, wqkv_d, 8, 3072, stages)
        c.barrier()
        c.release([st[1] for st in stages])
        xss = Rot([arena.alloc((8, 512), F32) for _ in range(2)])
        sq, sqb = arena.alloc((8, 512), BF16)
        rstd, rstdb = arena.alloc((512,), F32)
        tts = Rot([arena.alloc((512,), F32) for _ in range(3)])
        hs = Rot([arena.alloc((8, 512), BF16) for _ in range(2)])
        qos = Rot([arena.alloc((8, 512), BF16) for _ in range(2)])
        kos = Rot([arena.alloc((8, 512), BF16) for _ in range(2)])
        vos = Rot([arena.alloc((8, 4, 128), BF16) for _ in range(2)])
        psrot = Rot([0, 1, 2, 3, 4, 5, 6, 7])
        tr = lambda d_, t0: d_[:, t0:t0 + NT].rearrange("(k p) t -> p k t", p=128)
        def p3_a1(i):
            t0 = i * NT
            xs, xsb = xss.next()
            c.dma("sp", xs, tr(scr["x2T"], t0), [], [xsb], xsb)
            h_ap, hb = hs.next()
            return {"i": i, "t0": t0, "xs": xs, "xsb": xsb, "h": h_ap, "hb": hb}

        def p3_b(st):
            t0, h_ap, hb = st["t0"], st["h"], st["hb"]
            qo, qob = qos.next()
            ko, kob = kos.next()
            for m in range(16):
                pq = psrot.next()
                mm_group(pq, 512, [wq[:, k, m * 128:(m + 1) * 128] for k in range(KC)], [h_ap[:, k, :] for k in range(KC)], [hb])
                if m < 8:
                    c.op("act", lambda e, pq=pq, m=m, qo=qo: e.activation(out=qo[:, m, :], in_=bank[pq], func=AF.Copy, scale=0.125), [bankb[pq]], [qob])
                else:
                    c.op("dve", lambda e, pq=pq, m=m, ko=ko: e.tensor_copy(out=ko[:, m - 8, :], in_=bank[pq]), [bankb[pq]], [kob])
            c.dma(STORE_Q, tr(scr["qT"], t0), qo, [qob], [], qob)
            c.dma(STORE_Q, tr(scr["kT"], t0), ko, [kob], [], kob)
            st["vo"] = vos.next()

        def p3_c(st, n0, n1):
            i, h_ap, hb = st["i"], st["h"], st["hb"]
            vo, vob = st["vo"]
            for n in range(n0, n1):
                for hf in range(2):
                    pvv = psrot.next()
                    mm_group(pvv, 512, [h_ap[:, k, n * 128:(n + 1) * 128] for k in range(KC)],
                             [wq[:, k, 2048 + hf * 512:2048 + (hf + 1) * 512] for k in range(KC)], [hb])
                    src = bank[pvv].rearrange("p (a b) -> p a b", a=4)
                    if hf == 0:
                        c.op("act", lambda e, src=src, vo=vo, n=n, hf=hf: e.activation(out=vo[:, hf * 4:(hf + 1) * 4, n, :], in_=src, func=AF.Copy), [bankb[pvv]], [vob])
                    else:
                        c.op("dve", lambda e, src=src, vo=vo, n=n, hf=hf: e.tensor_copy(out=vo[:, hf * 4:(hf + 1) * 4, n, :], in_=src), [bankb[pvv]], [vob])
            if n1 == 4:
                c.dma(STORE_Q, scr["Vs"][:, :, 4 * i:4 * i + 4, :].rearrange("h p n e -> p h n e"), vo, [vob], [], vob)

        cur = p3_a1(0)
        norm_h(cur["xs"], cur["xsb"], 512, 2, sq, sqb, rstd, rstdb, tts, cur["h"], cur["hb"], psrot)
        for i in range(ntiles):
            p3_b(cur)
            nxt = None
            if i + 1 < ntiles:
                nxt = p3_a1(i + 1)
                norm_h(nxt["xs"], nxt["xsb"], 512, 2, sq, sqb, rstd, rstdb, tts, nxt["h"], nxt["hb"], psrot,
                       mid=lambda cur=cur: p3_c(cur, 0, 2))
            else:
                p3_c(cur, 0, 2)
            p3_c(cur, 2, 4)
            cur = nxt
        c.barrier()
        c.release([b for (_, b) in xss.items] + [b for (_, b) in qos.items] + [b for (_, b) in kos.items] + [b for (_, b) in vos.items])

    if "p4" in phases:
        arena.reset()
        NKC = S // 128
        kTs = [arena.alloc((2, S), BF16) for _ in range(2)]
        qTs = [arena.alloc((S,), BF16) for _ in range(2)]
        Vhs = [arena.alloc((NKC, 128), BF16) for _ in range(2)]
        bths = [arena.alloc((NDELTA, 512), BF16) for _ in range(2)]
        btst = Rot([arena.alloc((512,), F32) for _ in range(2)])
        pts = Rot([arena.alloc((2, 512), BF16) for _ in range(4)])
        osqs = Rot([arena.alloc((512,), BF16) for _ in range(2)])
        ons = Rot([arena.alloc((512,), BF16) for _ in range(2)])
        for (kt_, kb_) in kTs:
            c.op("pool", lambda e, kt_=kt_: e.memset(kt_, 0.0), [], [kb_])
        srot = Rot([0, 1])
        Sb = [Buf(), Buf()]
        O1, O2, S1, S2 = 4, 5, 6, 7
        kb1s = [Buf(), Buf()]
        cb0 = BC["cb"]
        FIN_DELAY = 14
        pending = []

        def emit_qk(it):
            if "fin" in it:
                si = srot.next()
                it["si"] = si
                Sp = pp[si]
                osq, osqb = it["fin"][6]
                c.op("pe", lambda e: e.matmul(Sp[:, 0, :], ones_bf[:], osq, start=True, stop=True), [osqb, b_ones], [Sb[si]])
                return
            hh, qt, kc, bufs = it["hh"], it["qt"], it["kc"], it["bufs"]
            kTh, kb, kb1, qTh, qb, Vh, vb, bth, bb = bufs
            q0 = qt * NT
            delta = kc - 4 * qt
            near = (-1 <= delta <= NDELTA - 2)
            si = srot.next()
            Sp = pp[si]
            it["si"], it["near"], it["delta"] = si, near, delta

            def fqk(e):
                e.matmul(Sp[:, 0, :], kTh[:, 0, kc * 128:(kc + 1) * 128], qTh[:, q0:q0 + NT], start=True, stop=not near)
                ins = e.matmul(Sp[:, 1, :], kTh[:, 1, kc * 128:(kc + 1) * 128], qTh[:, q0:q0 + NT], start=True, stop=not near)
                if near:
                    e.matmul(Sp[:, 0, :], ident_bf[:], bth[:, delta + 1, :], start=False, stop=True)
                    ins = e.matmul(Sp[:, 1, :], ident_bf[:], bth[:, delta + 1, :], start=False, stop=True)
                return ins
            c.op("pe", fqk, [kb, kb1, qb, bb, b_identb] if near else [kb, kb1, qb], [Sb[si]])

        def emit_exp(it):
            if "fin" in it:
                (s1, s1b), (s2, s2b), (t1, t1b), (t2, t2b), (ot, otb), (rs, rsb), _osq = it["fin"]
                hh, qt = it["hh"], it["qt"]
                q0 = qt * NT
                Sp = pp[it["si"]]
                c.op("act", lambda e: e.activation(out=rs, in_=Sp[:, 0, :], func=AF.Ln, scale=1.0 / 128, bias=LN_EPS), [Sb[it["si"]]], [rsb])
                c.op("act", lambda e: e.activation(out=rs, in_=rs, func=AF.Exp, scale=-0.5), [rsb], [rsb])
                on, onb = ons.next()
                c.op("dve", lambda e: e.scalar_tensor_tensor(out=on, in0=ot, scalar=SUBLNG, in1=rs, op0=ALU.mult, op1=ALU.mult), [otb, rsb, b_small], [onb])
                c.dma(STORE_Q, scr["oT"][hh * 128:(hh + 1) * 128, q0:q0 + NT], on, [onb], [], onb)
                return
            hh, qt, kc, bufs = it["hh"], it["qt"], it["kc"], it["bufs"]
            kTh, kb, kb1, qTh, qb, Vh, vb, bth, bb = bufs
            si, near, delta = it["si"], it["near"], it["delta"]
            Sp = pp[si]
            pt, ptb = pts.next()
            if near:
                bias = 0.0
            else:
                col = cb0 + hh * 2 + (0 if delta < 0 else 1)
                bias = bcs[:, col:col + 1]
            c.op("act", lambda e: e.activation(out=pt.rearrange("p a b -> p (a b)"), in_=Sp[:].rearrange("p a b -> p (a b)"),
                                               func=AF.Exp, bias=bias), [Sb[si], b_bc], [ptb])
            it["pt"] = (pt, ptb)

        def emit_pv(it):
            if "fin" in it:
                return
            hh, qt, kc, bufs = it["hh"], it["qt"], it["kc"], it["bufs"]
            kTh, kb, kb1, qTh, qb, Vh, vb, bth, bb = bufs
            pt, ptb = it["pt"]

            if kc == 0:
                c.op("dve", lambda e: e.tensor_copy(out=bank[S1], in_=pt[:, 0, :]), [ptb], [bankb[S1]])
            else:
                c.op("dve", lambda e: e.tensor_tensor(out=bank[S1], in0=bank[S1], in1=pt[:, 0, :], op=ALU.add), [ptb, bankb[S1]], [bankb[S1]])

            def fpv(e):
                st_, sp_ = (kc == 0), (kc == NKC - 1)
                e.matmul(bank[O1], Vh[:, kc, :], pt[:, 0, :], start=st_, stop=sp_)
                e.matmul(bank[O2], Vh[:, kc, :], pt[:, 1, :], start=st_, stop=sp_)
                return e.matmul(bank[S2], ones_bf[:], pt[:, 1, :], start=st_, stop=sp_)
            c.op("pe", fpv, [ptb, vb, b_ones], [bankb[O1], bankb[O2], bankb[S2]])
            if kc == NKC - 1:
                emit_fin_a(hh, qt)

        fins = Rot([[arena.alloc((512,), F32) for _ in range(6)] for _ in range(2)])

        def emit_fin_a(hh, qt):
            q0 = qt * NT
            fs = list(fins.next())
            (s1, s1b), (s2, s2b), (t1, t1b), (t2, t2b), (ot, otb), (rs, rsb) = fs
            osq, osqb = osqs.next()
            c.op("dve", lambda e: e.tensor_copy(out=osq, in_=bank[S1]), [bankb[S1]], [osqb])
            c.op("dve", lambda e: e.tensor_copy(out=t1, in_=bank[O1]), [bankb[O1]], [t1b])
            c.op("dve", lambda e: e.tensor_copy(out=t2, in_=bank[O2]), [bankb[O2]], [t2b])
            c.op("pe", lambda e: e.matmul(bank[S1], ones_bf[:], osq, start=True, stop=True), [osqb, b_ones], [bankb[S1]])
            c.op("act", lambda e: e.activation(out=s1, in_=bank[S1], func=AF.Ln), [bankb[S1]], [s1b])
            c.op("act", lambda e: e.activation(out=s1, in_=s1, func=AF.Exp, scale=-1.0), [s1b], [s1b])
            c.op("act", lambda e: e.activation(out=s2, in_=bank[S2], func=AF.Ln), [bankb[S2]], [s2b])
            c.op("act", lambda e: e.activation(out=s2, in_=s2, func=AF.Exp, scale=-1.0), [s2b], [s2b])
            c.op("dve", lambda e: e.tensor_tensor(out=t1, in0=t1, in1=s1, op=ALU.mult), [t1b, s1b], [t1b])
            c.op("dve", lambda e: e.tensor_tensor(out=t2, in0=t2, in1=s2, op=ALU.mult), [t2b, s2b], [t2b])
            c.op("dve", lambda e: e.scalar_tensor_tensor(out=ot, in0=t2, scalar=NEGLAM, in1=t1, op0=ALU.mult, op1=ALU.add), [t1b, t2b, b_small], [otb])
            c.op("dve", lambda e: e.tensor_tensor(out=osq, in0=ot, in1=ot, op=ALU.mult), [otb], [osqb])
            pending.append([FIN_DELAY, {"fin": fs + [(osq, osqb)], "hh": hh, "qt": qt}])

        inflight = []

        def step(it):
            if len(inflight) >= 2:
                emit_exp(inflight[0])
            emit_qk(it)
            if len(inflight) >= 2:
                emit_pv(inflight.pop(0))
            inflight.append(it)

        for hh in range(nheads):
            kTh, kb = kTs[hh % 2]
            qTh, qb = qTs[hh % 2]
            Vh, vb = Vhs[hh % 2]
            bth, bb = bths[hh % 2]
            kb1 = kb1s[hh % 2]
            c.dma("sp", kTh[0:64, 0, :], scr["kT"][hh * 128:hh * 128 + 64, :], [], [kb], kb)
            c.dma("sp", kTh[64:128, 1, :], scr["kT"][hh * 128 + 64:(hh + 1) * 128, :], [kb], [kb1], kb1)
            c.dma("sp", qTh, scr["qT"][hh * 128:(hh + 1) * 128, :], [], [qb], qb)
            c.dma("sp", Vh, scr["Vs"][hh], [], [vb], vb)
            for di in range(NDELTA):
                st_, stb_ = btst.next()
                c.dma("sp", st_, bt_d[hh, di], [], [stb_], stb_)
                c.op("pool", lambda e, st_=st_, bth=bth, di=di: e.tensor_copy(out=bth[:, di, :], in_=st_), [stb_], [bb])
            bufs = (kTh, kb, kb1, qTh, qb, Vh, vb, bth, bb)
            for qt in range(ntiles):
                for kc in range(NKC):
                    step({"hh": hh, "qt": qt, "kc": kc, "bufs": bufs})
                    for p_ in pending:
                        p_[0] -= 1
                    if pending and pending[0][0] <= 0:
                        step(pending.pop(0)[1])
        while True:
            if pending:
                step(pending.pop(0)[1])
            elif inflight:
                it = inflight.pop(0)
                emit_exp(it)
                emit_pv(it)
            else:
                break
        c.barrier()
        c.release(kb1s + [b for (_, b) in kTs + qTs + Vhs] + [b for (_, b) in btst.items] + [b for (_, b) in ons.items])

    if "p6a" in phases:
        mlp_pass(3, 1, 0, None, None, scr["xbT"], pre=(scr["oT"], scr["x2T"], scr["x3T"], 2, wao_d))
    if "p6b" in phases:
        mlp_pass(3, 1, 1, scr["x3T"], scr["xbT"], outT, final=True)

    c.barrier()
    c.emit()
    return nc


def _t5_bucket_np(rel):
    import jax
    import jax.numpy as jnp
    nb = 16
    max_exact = 8
    with jax.default_device(jax.devices("cpu")[0]):
        rel = jnp.asarray(rel, dtype=jnp.int32)
        ret = jnp.where(rel > 0, nb, 0).astype(jnp.int32)
        n = jnp.abs(rel)
        nf = jnp.maximum(n, 1).astype(jnp.float32)
        large = max_exact + (jnp.log(nf / max_exact) / math.log(128 / max_exact) * (nb - max_exact)).astype(jnp.int32)
        large = jnp.minimum(large, nb - 1)
        return np.asarray(ret + jnp.where(n < max_exact, n, large))


def _fm(v, ncol):
    return np.ascontiguousarray(np.asarray(v, np.float32).reshape(ncol, 128).T)


def prep_shared(inp):
    g = lambda k: np.asarray(inp[k], np.float32)
    sh = {}
    sh["modw0"] = np.ascontiguousarray(g("mix_mod_w")[0])
    sh["modw1"] = np.ascontiguousarray(g("mlp_mod_w")[0])
    sh["modw2"] = np.ascontiguousarray(g("mix_mod_w")[1])
    sh["modw3"] = np.ascontiguousarray(g("mlp_mod_w")[1])
    sh["w_in"] = np.ascontiguousarray(g("ab_w_in")[0])
    sh["w_out"] = np.ascontiguousarray(g("ab_w_out")[0])
    sh["wqkv"] = np.ascontiguousarray(g("attn_w_qkv")[0])
    sh["wao"] = np.ascontiguousarray(g("attn_w_out")[0])
    for i in range(2):
        sh["w1_%d" % i] = np.ascontiguousarray(g("mlp_w1")[i])
        sh["w2_%d" % i] = np.ascontiguousarray(g("mlp_w2")[i])
    sh["ident"] = np.eye(128, dtype=np.float32)
    sh["sgwT"] = np.ascontiguousarray(np.transpose(g("sg_w")[0], (2, 0, 1)).reshape(128, 512))
    rel = np.arange(-(S - 1), S, dtype=np.int32)
    bucket = _t5_bucket_np(rel)
    tab = g("rel_bias_table")
    assert len(set(bucket[: S - 1 - 128].tolist())) == 1 and len(set(bucket[S - 1 + 129:].tolist())) == 1
    bt = np.empty((H, NDELTA, 128, 512), np.float32)
    ii = np.arange(128)[:, None]
    jj = np.arange(512)[None, :]
    for di in range(NDELTA):
        relm = 128 * (di - 1) + ii - jj
        bk = bucket[relm + S - 1]
        bt[:, di] = np.transpose(tab[bk], (2, 0, 1))
    sh["bt"] = bt
    cb = np.empty((H, 2), np.float32)
    cb[:, 0] = tab[bucket[0]]
    cb[:, 1] = tab[bucket[-1]]
    pvs = np.zeros((128, NPV), np.float32)

    def put(name, v, ncol):
        pvs[:, PV[name]:PV[name] + ncol] = _fm(v, ncol)
    put("mixg0", g("mix_norm_g")[0], 8)
    put("mlpg0", g("mlp_norm_g")[0], 8)
    put("mixg1", g("mix_norm_g")[1], 8)
    put("mlpg1", g("mlp_norm_g")[1], 8)
    put("fing", g("final_norm_g"), 8)
    put("modb0", g("mix_mod_b")[0], 24)
    put("modb1", g("mlp_mod_b")[0], 24)
    put("modb2", g("mix_mod_b")[1], 24)
    put("modb3", g("mlp_mod_b")[1], 24)
    put("convb", g("conv_b")[0], 4)
    put("clng", g("conv_ln_g")[0], 4)
    put("clnb", g("conv_ln_b")[0], 4)
    put("sgg", g("sg_ln_g")[0], 4)
    put("sublng", g("subln_g")[0], 1)
    cw = g("conv_w")[0]
    pvs[:, PV["convw"]:PV["convw"] + 124] = np.transpose(cw.reshape(31, 4, 128), (2, 1, 0)).reshape(128, 124)
    bcv = np.zeros((128, NBC), np.float32)
    bcv[:, 0:64] = g("lam_q1")[0][None]
    bcv[:, 64:128] = g("lam_k1")[0][None]
    bcv[:, 128:192] = g("lam_q2")[0][None]
    bcv[:, 192:256] = g("lam_k2")[0][None]
    bcv[:, BC["sgbb"]:BC["sgbb"] + 512] = g("sg_b")[0].reshape(1, 512)
    bcv[:, BC["bmat"]:BC["bmat"] + 512] = g("sg_ln_b")[0].reshape(1, 512)
    bcv[:, BC["cb"]:BC["cb"] + 16] = cb.reshape(1, 16)
    sh["bc"] = bcv
    return sh, pvs


def core_inputs(inp, b, sh, pvs):
    m = dict(sh)
    pvb = pvs.copy()
    pvb[:, PV["c"]:PV["c"] + 8] = _fm(np.asarray(inp["c"], np.float32)[b], 8)
    m["pv"] = pvb
    m["xT"] = np.ascontiguousarray(np.asarray(inp["x"], np.float32)[b].T)
    return m


_NC_CACHE = {}


def kernel(**inputs):
    sh, pvs = prep_shared(inputs)
    B = np.asarray(inputs["x"]).shape[0]
    if "full" not in _NC_CACHE:
        _NC_CACHE["full"] = build()
    nc = _NC_CACHE["full"]
    in_maps = [core_inputs(inputs, b, sh, pvs) for b in range(B)]
    res = run_bass_kernel_spmd(nc, in_maps, core_ids=list(range(B)))
    out = np.empty((B, S, D), np.float32)
    for b in range(B):
        out[b] = res.results[b]["outT"].T
    return out
```

```python
import math
import numpy as np
import concourse.bass as bass
import concourse.mybir as mybir
from concourse.bass_utils import run_bass_kernel_spmd

F32 = mybir.dt.float32
BF16 = mybir.dt.bfloat16
AF = mybir.ActivationFunctionType
ALU = mybir.AluOpType
AX = mybir.AxisListType

S = 8192
D = 1024
KC = 8
NT = 512
NTILES = S // NT
H = 8
DFF = 4096
RMS_EPS = 1e-6
LN_EPS = 1e-5
LAMBDA_INIT = 0.8 - 0.6 * math.exp(-0.3 * 1)
NDELTA = 6
STORE_Q = "pool"

PV = {}
_o = 0
for _n, _w in (("c", 8), ("mixg0", 8), ("mlpg0", 8), ("mixg1", 8), ("mlpg1", 8), ("fing", 8),
               ("modb0", 24), ("modb1", 24), ("modb2", 24), ("modb3", 24),
               ("convb", 4), ("clng", 4), ("clnb", 4), ("sgg", 4), ("sublng", 1), ("convw", 124)):
    PV[_n] = _o
    _o += _w
NPV = _o
BC = {"lam": 0, "sgbb": 256, "bmat": 768, "cb": 1280}
NBC = 1280 + 16


class Sem:
    __slots__ = ("h", "count")

    def __init__(self, h):
        self.h = h
        self.count = 0


class Buf:
    __slots__ = ("name", "w", "r", "sem")

    def __init__(self, name=""):
        self.name = name
        self.w = None
        self.r = []
        self.sem = None


class _FirstRec:
    def __init__(self, e):
        self.e = e
        self.first = None

    def __getattr__(self, name):
        f = getattr(self.e, name)

        def w(*a, **k):
            r = f(*a, **k)
            if self.first is None:
                self.first = r
            return r
        return w


class Ctx:
    def __init__(self, nc):
        self.nc = nc
        self.engs = {"pe": nc.tensor, "act": nc.scalar, "dve": nc.vector,
                     "pool": nc.gpsimd, "sp": nc.sync}
        self.prog = {e: [] for e in self.engs}
        self.esem = {e: Sem(nc.alloc_semaphore("es_" + e)) for e in ("pe", "act", "dve", "pool")}
        self.seen = {e: {} for e in self.engs}
        self.free_sems = []
        self.all_dsems = []

    def dsem(self):
        if self.free_sems:
            return self.free_sems.pop()
        s = Sem(self.nc.alloc_semaphore("ds%d" % len(self.all_dsems)))
        self.all_dsems.append(s)
        return s

    def release(self, bufs):
        for b in bufs:
            if b.sem is not None:
                self.free_sems.append(b.sem)
                b.sem = None

    def _deps(self, eng, reads, writes):
        deps = {}
        for b in reads:
            if b.w is not None:
                s, v = b.w
                if deps.get(s, 0) < v:
                    deps[s] = v
        for b in writes:
            if b.w is not None:
                s, v = b.w
                if deps.get(s, 0) < v:
                    deps[s] = v
            for (s, v) in b.r:
                if deps.get(s, 0) < v:
                    deps[s] = v
        waits = []
        seen = self.seen[eng]
        own = self.esem.get(eng)
        for s, v in deps.items():
            if seen.get(s, 0) >= v:
                continue
            if s is own and (eng == "pe" or s.count - v >= 4):
                continue
            seen[s] = v
            waits.append((s.h, v))
        return waits

    def op(self, eng, fn, reads=(), writes=()):
        waits = self._deps(eng, reads, writes)
        s = self.esem[eng]
        s.count += 1
        done = (s, s.count)
        for b in reads:
            b.r.append(done)
        for b in writes:
            b.w = done
            b.r = []
        self.prog[eng].append((waits, fn, s.h, 1))

    def dma(self, q, out_ap, in_ap, reads, writes, sembuf):
        waits = self._deps(q, reads, writes)
        if sembuf.sem is None:
            sembuf.sem = self.dsem()
        s = sembuf.sem
        s.count += 16
        done = (s, s.count)
        for b in reads:
            b.r.append(done)
        for b in writes:
            b.w = done
            b.r = []

        def fn(e, out_ap=out_ap, in_ap=in_ap):
            return e.dma_start(out=out_ap, in_=in_ap)
        self.prog[q].append((waits, fn, s.h, 16))

    def barrier(self):
        targets = [(s, s.count) for s in self.esem.values() if s.count > 0]
        targets += [(s, s.count) for s in self.all_dsems if s.count > 0]
        for e in self.engs:
            seen = self.seen[e]
            waits = []
            for s, v in targets:
                if seen.get(s, 0) >= v:
                    continue
                seen[s] = v
                waits.append((s.h, v))
            if waits:
                self.prog[e].append((waits, None, None, 0))

    def emit(self):
        nc = self.nc
        with nc.Block() as block:
            def mk(name):
                def body(e):
                    for waits, fn, sh, inc in self.prog[name]:
                        if fn is None:
                            for (h, v) in waits:
                                e.wait_ge(h, v)
                            continue
                        for (h, v) in waits:
                            e.wait_ge(h, v)
                        r = fn(e)
                        last = r[1] if isinstance(r, tuple) else r
                        last.then_inc(sh, inc)
                return body
            block.sync(mk("sp"))
            block.scalar(mk("act"))
            block.vector(mk("dve"))
            block.gpsimd(mk("pool"))
            block.tensor(mk("pe"))


class Arena:
    def __init__(self, nc, nbytes):
        self.n = nbytes // 2
        self.t = nc.alloc_sbuf_tensor("arena", [128, self.n], BF16)
        self.off = 0

    def reset(self):
        self.off = 0

    def alloc(self, shape, dt):
        n = int(np.prod(shape))
        units = n * (2 if dt == F32 else 1)
        self.off = (self.off + 15) // 16 * 16
        assert self.off + units <= self.n, ("arena overflow", self.off, units, self.n)
        ap = self.t[:, self.off:self.off + units]
        self.off += units
        if dt == F32:
            ap = ap.bitcast(F32)
        if len(shape) == 2:
            ap = ap.rearrange("p (a b) -> p a b", a=shape[0])
        elif len(shape) == 3:
            ap = ap.rearrange("p (a b c) -> p a b c", a=shape[0], b=shape[1])
        return ap, Buf()


class Rot:
    def __init__(self, items):
        self.items = items
        self.i = 0

    def next(self):
        it = self.items[self.i % len(self.items)]
        self.i += 1
        return it


ALL_PHASES = ("p1", "p2a", "p2b", "p3", "p4", "p6a", "p6b")
SCRATCH = {
    "mixT": ([D, S], BF16, "p1"),
    "x1T": ([D, S], F32, "p2a"),
    "xaT": ([D, S], F32, "p2a"),
    "x2T": ([D, S], F32, "p2b"),
    "qT": ([D, S], BF16, "p3"),
    "kT": ([D, S], BF16, "p3"),
    "Vs": ([H, 128, S // 128, 128], BF16, "p3"),
    "oT": ([D, S], BF16, "p4"),
    "x3T": ([D, S], F32, "p6a"),
    "xbT": ([D, S], F32, "p6a"),
}
CONSUMES = {
    "p1": [], "p2a": ["mixT"], "p2b": ["x1T", "xaT"], "p3": ["x2T"], "p4": ["qT", "kT", "Vs"],
    "p6a": ["oT", "x2T"], "p6b": ["x3T", "xbT"],
}


def build(phases=ALL_PHASES, ntiles=NTILES, nheads=H, debug_out=False):
    nc = bass.Bass("TRN2", target_bir_lowering=False)
    phases = tuple(phases)

    def din(name, shape, dt=F32):
        return nc.dram_tensor(name, list(shape), dt, kind="ExternalInput").ap()

    xT = din("xT", [D, S])
    pv_d = din("pv", [128, NPV])
    bc_d = din("bc", [128, NBC])
    sgwT_d = din("sgwT", [128, 512])
    ident_d = din("ident", [128, 128])
    modw_d = [din("modw%d" % i, [D, 3 * D]) for i in range(4)]
    w_in_d = din("w_in", [D, 2048])
    w_out_d = din("w_out", [D, D])
    wqkv_d = din("wqkv", [D, 3 * D])
    wao_d = din("wao", [D, D])
    w1_d = [din("w1_%d" % i, [D, DFF]) for i in range(2)]
    w2_d = [din("w2_%d" % i, [DFF, D]) for i in range(2)]
    bt_d = din("bt", [H, NDELTA, 128, 512])
    scr = {}
    for name, (shape, dt, prod) in SCRATCH.items():
        produced = prod in phases
        consumed = any(name in CONSUMES[p] for p in phases)
        if produced:
            later = any(name in CONSUMES[p] for p in ALL_PHASES if p not in phases)
            kind = "ExternalOutput" if (later or debug_out) else "Internal"
        elif consumed:
            kind = "ExternalInput"
        else:
            continue
        scr[name] = nc.dram_tensor(name, list(shape), dt, kind=kind).ap()
    outT = None
    if "p6b" in phases:
        outT = nc.dram_tensor("outT", [D, S], F32, kind="ExternalOutput").ap()

    c = Ctx(nc)
    dbg_bufs = []

    def dbg_dump(name, ap, buf, dt=F32):
        if not debug_out:
            return
        shp = [int(x) for x in ap.shape]
        d_ = nc.dram_tensor("dbg_" + name, shp, dt, kind="ExternalOutput").ap()
        b_ = Buf()
        c.dma("sp", d_, ap, [buf], [], b_)
        dbg_bufs.append(b_)
    pv = nc.alloc_sbuf_tensor("pv_sb", [128, NPV], F32)
    bcs = nc.alloc_sbuf_tensor("bc_sb", [128, NBC], F32)
    modv = nc.alloc_sbuf_tensor("modv", [128, 96], F32)
    coefA = nc.alloc_sbuf_tensor("coefA", [128, 32], F32)
    cact = nc.alloc_sbuf_tensor("cact", [128, 8], F32)
    ones_bf = nc.alloc_sbuf_tensor("ones_bf", [128, 128], BF16)
    ones_f = nc.alloc_sbuf_tensor("ones_f", [128, 128], F32)
    b_onesf = Buf()
    ident_f = nc.alloc_sbuf_tensor("ident_f", [128, 128], F32)
    ident_bf = nc.alloc_sbuf_tensor("ident_bf", [128, 128], BF16)
    small = nc.alloc_sbuf_tensor("small", [128, 16], F32)
    lamp = nc.alloc_sbuf_tensor("lamp", [128, 128], F32)
    b_pv, b_bc, b_modv, b_coef, b_cact, b_ones, b_identf, b_identb, b_small, b_lamp = [Buf() for _ in range(10)]
    arena = Arena(nc, 199 * 1024)
    pp = [nc.alloc_psum_tensor("pp%d" % i, [128, 2, 512], F32) for i in range(4)]
    bank = [pp[i // 2][:, i % 2, :] for i in range(8)]
    bankb = [Buf("bank%d" % i) for i in range(8)]

    GATE = lambda s, m: modv[:, s * 24 + 16 + m: s * 24 + 17 + m]
    SHIFT = lambda s, m: modv[:, s * 24 + m: s * 24 + m + 1]
    COEF = lambda s, m: coefA[:, s * 8 + m: s * 8 + m + 1]
    NORMG = {0: "mixg0", 1: "mlpg0", 2: "mixg1", 3: "mlpg1"}

    c.dma("sp", pv[:], pv_d, [], [b_pv], b_pv)
    c.dma("sp", bcs[:], bc_d, [], [b_bc], b_bc)
    c.dma("sp", ident_f[:], ident_d, [], [b_identf], b_identf)
    c.op("pool", lambda e: e.memset(ones_bf[:], 1.0), [], [b_ones])
    c.op("pool", lambda e: e.memset(ones_f[:], 1.0), [], [b_onesf])
    c.op("pool", lambda e: e.tensor_copy(out=ident_bf[:], in_=ident_f[:]), [b_identf], [b_identb])
    c.op("act", lambda e: e.activation(out=cact[:], in_=pv[:, PV["c"]:PV["c"] + 8], func=AF.Silu), [b_pv], [b_cact])
    L = BC["lam"]
    c.op("dve", lambda e: e.tensor_tensor(out=lamp[:, 0:64], in0=bcs[:, L:L + 64], in1=bcs[:, L + 64:L + 128], op=ALU.mult), [b_bc], [b_lamp])
    c.op("dve", lambda e: e.tensor_tensor(out=lamp[:, 64:128], in0=bcs[:, L + 128:L + 192], in1=bcs[:, L + 192:L + 256], op=ALU.mult), [b_bc], [b_lamp])
    c.op("dve", lambda e: e.tensor_reduce(out=small[:, 0:2], in_=lamp[:].rearrange("p (a b) -> p a b", a=2), axis=AX.X, op=ALU.add), [b_lamp], [b_small])
    c.op("act", lambda e: e.activation(out=small[:, 2:4], in_=small[:, 0:2], func=AF.Exp), [b_small], [b_small])
    c.op("dve", lambda e: e.tensor_tensor(out=small[:, 4:5], in0=small[:, 3:4], in1=small[:, 2:3], op=ALU.subtract), [b_small], [b_small])
    c.op("dve", lambda e: e.tensor_scalar(out=small[:, 4:5], in0=small[:, 4:5], scalar1=-LAMBDA_INIT, scalar2=None, op0=ALU.add), [b_small], [b_small])
    c.op("dve", lambda e: e.tensor_scalar(out=small[:, 5:6], in0=pv[:, PV["sublng"]:PV["sublng"] + 1], scalar1=1.0 - LAMBDA_INIT, scalar2=None, op0=ALU.mult), [b_pv, b_small], [b_small])
    NEGLAM = small[:, 4:5]
    SUBLNG = small[:, 5:6]

    arena.reset()
    wst = [arena.alloc((8, 1536), F32) for _ in range(2)]
    mps = pp[0][:, 0, 0:96]
    for s in range(4):
        for half in range(2):
            wb, wbuf = wst[(s * 2 + half) % 2]
            src = modw_d[s][:, half * 1536:(half + 1) * 1536].rearrange("(k p) n -> p k n", p=128)
            c.dma("sp", wb, src, [], [wbuf], wbuf)

            def f(e, wb=wb, s=s, half=half):
                ins = None
                for j in range(12):
                    col = s * 24 + half * 12 + j
                    for k in range(KC):
                        ins = e.matmul(mps[:, col:col + 1], wb[:, k, j * 128:(j + 1) * 128], cact[:, k:k + 1],
                                       start=(k == 0), stop=(k == KC - 1))
                return ins
            c.op("pe", f, [wbuf, b_cact], [bankb[0]])
    for s in range(4):
        o = PV["modb%d" % s]
        c.op("dve", lambda e, s=s, o=o: e.tensor_tensor(out=modv[:, s * 24:(s + 1) * 24], in0=mps[:, s * 24:(s + 1) * 24],
                                                       in1=pv[:, o:o + 24], op=ALU.add), [bankb[0], b_pv], [b_modv])
        g = PV[NORMG[s]]
        c.op("dve", lambda e, s=s, g=g: e.scalar_tensor_tensor(out=coefA[:, s * 8:(s + 1) * 8], in0=modv[:, s * 24 + 8:s * 24 + 16],
                                                              scalar=1.0, in1=pv[:, g:g + 8], op0=ALU.add, op1=ALU.mult),
             [b_modv, b_pv], [b_coef])
    c.barrier()
    c.release([wst[0][1], wst[1][1]])

    stage_rot_engines = ("pool", "dve")

    def load_w_bf16(dst, src, kc, ncols, stages, cnt=[0]):
        step = 2048
        for k in range(kc):
            for n0 in range(0, ncols, step):
                w = min(step, ncols - n0)
                st, sb = stages[cnt[0] % len(stages)]
                eng = stage_rot_engines[cnt[0] % 2]
                cnt[0] += 1
                c.dma("sp", st[:, 0:w], src[k * 128:(k + 1) * 128, n0:n0 + w], [], [sb], sb)
                tmpb = Buf()
                c.op(eng, lambda e, st=st, k=k, n0=n0, w=w: e.tensor_copy(out=dst[:, k, n0:n0 + w], in_=st[:, 0:w]), [sb], [tmpb])

    def rstd_from_psum(ps_ap, rstd_ap, n, scale, eps, reads, writes):
        c.op("act", lambda e: e.activation(out=rstd_ap, in_=ps_ap, func=AF.Ln, scale=scale, bias=eps), reads, writes)
        c.op("act", lambda e: e.activation(out=rstd_ap, in_=rstd_ap, func=AF.Exp, scale=-0.5), writes, writes)

    def norm_sq(x_ap, xb, W, sq, sqb):
        c.op("act", lambda e: e.activation(out=sq[:, :, 0:W], in_=x_ap, func=AF.Square), [xb], [sqb])

    def norm_rest(x_ap, xb, W, s, sq, sqb, rstd, rstdb, tts, h_ap, hb, psrot):
        groups = [(0, min(512, W))] + ([(512, W)] if W > 512 else [])
        for (c0, c1) in groups:
            pi = psrot.next()
            n = c1 - c0

            def f(e, pi=pi, c0=c0, c1=c1, n=n):
                ins = None
                for k in range(KC):
                    ins = e.matmul(bank[pi][:, 0:n], ones_bf[:], sq[:, k, c0:c1], start=(k == 0), stop=(k == KC - 1))
                return ins
            c.op("pe", f, [sqb, b_ones], [bankb[pi]])
            rstd_from_psum(bank[pi][:, 0:n], rstd[:, c0:c1], n, 1.0 / D, RMS_EPS, [bankb[pi]], [rstdb])
        for k in range(KC):
            t, tb = tts.next()
            c.op("dve", lambda e, t=t, k=k: e.tensor_tensor(out=t[:, 0:W], in0=x_ap[:, k, :], in1=rstd[:, 0:W], op=ALU.mult), [xb, rstdb], [tb])
            c.op("act", lambda e, t=t, k=k: e.activation(out=h_ap[:, k, 0:W], in_=t[:, 0:W], func=AF.Identity,
                                                        scale=COEF(s, k), bias=SHIFT(s, k)), [tb, b_coef, b_modv], [hb])

    def norm_h(x_ap, xb, W, s, sq, sqb, rstd, rstdb, tts, h_ap, hb, psrot, mid=None):
        norm_sq(x_ap, xb, W, sq, sqb)
        if mid is not None:
            mid()
        norm_rest(x_ap, xb, W, s, sq, sqb, rstd, rstdb, tts, h_ap, hb, psrot)

    def mm_group(pi, n, lhs_list, rhs_list, reads):
        def f(e):
            ins = first = None
            L_ = len(lhs_list)
            for i in range(L_):
                ins = e.matmul(bank[pi][:, 0:n], lhs_list[i], rhs_list[i], start=(i == 0), stop=(i == L_ - 1))
                if first is None:
                    first = ins
            return first, ins
        c.op("pe", f, reads, [bankb[pi]])

    if "p1" in phases:
        arena.reset()
        w_in_sb, _ = arena.alloc((8, 2048), BF16)
        dg, b_dg = arena.alloc((4, 31, 128), BF16)
        sgw_f, b_sgwf = arena.alloc((4, 128), F32)
        sgw_bf, b_sgwb = arena.alloc((4, 128), BF16)
        Rt, b_R = arena.alloc((4, 128), F32)
        mark = arena.off
        stages = [arena.alloc((2048,), F32) for _ in range(3)]
        load_w_bf16(w_in_sb, w_in_d, 8, 2048, stages)
        c.dma("sp", sgw_f, sgwT_d.rearrange("p (a b) -> p a b", a=4), [], [b_sgwf], b_sgwf)
        c.op("pool", lambda e: e.tensor_copy(out=sgw_bf, in_=sgw_f), [b_sgwf], [b_sgwb])
        for cc in range(4):
            o = PV["convw"] + cc * 31
            c.op("dve", lambda e, cc=cc, o=o: e.tensor_tensor(
                out=dg[:, cc, :, :], in0=ident_f[:].unsqueeze(1).broadcast_to([128, 31, 128]),
                in1=pv[:, o:o + 31].unsqueeze(2).broadcast_to([128, 31, 128]), op=ALU.mult), [b_identf, b_pv], [b_dg])
        bm = BC["bmat"]

        def fR(e):
            ins = None
            for g in range(4):
                ins = e.matmul(bank[1][:, g * 128:(g + 1) * 128], bcs[:, bm + g * 128:bm + (g + 1) * 128], sgw_f[:, g, :],
                               start=True, stop=True)
            return ins
        c.op("pe", fR, [b_bc, b_sgwf], [bankb[1]])
        c.op("dve", lambda e: e.tensor_tensor(out=Rt.rearrange("p a b -> p (a b)"), in0=bank[1], in1=bcs[:, BC["sgbb"]:BC["sgbb"] + 512],
                                              op=ALU.add), [bankb[1], b_bc], [b_R])
        c.barrier()
        c.release([st[1] for st in stages])
        arena.off = mark
        WX = 544
        xes = Rot([arena.alloc((8, WX), F32) for _ in range(2)])
        sq, sqb = arena.alloc((8, WX), BF16)
        rstd, rstdb = arena.alloc((WX,), F32)
        tts = Rot([arena.alloc((WX,), F32) for _ in range(3)])
        hs = Rot([arena.alloc((8, WX), BF16) for _ in range(1)])
        a_t, a_b = arena.alloc((4, WX), BF16)
        sgts = Rot([arena.alloc((512,), F32) for _ in range(2)])
        sgh, sghb = arena.alloc((4, 32), F32)
        yb, ybb = arena.alloc((4, 512), BF16)
        ysq, ysqb = arena.alloc((4, 512), BF16)
        mean_t, meanb = arena.alloc((512,), F32)
        msq_t, msqb = arena.alloc((512,), F32)
        rs2, rs2b = arena.alloc((512,), F32)
        dts = Rot([arena.alloc((512,), F32) for _ in range(2)])
        mixeds = Rot([arena.alloc((8, 512), BF16) for _ in range(2)])
        ug, ugb = arena.alloc((4, 512), BF16)
        gvs = Rot([arena.alloc((512,), F32) for _ in range(2)])
        st6, st6b = arena.alloc((8,), F32)
        mv, mvb = arena.alloc((4,), F32)
        vn, vnb = arena.alloc((4, 512), BF16)
        sgtmp = Rot([arena.alloc((512,), F32) for _ in range(2)])
        for (ap_, b_) in xes.items:
            c.op("pool", lambda e, ap_=ap_: e.memset(ap_, 0.0), [], [b_])
        psrot = Rot([0, 1, 2, 3, 4, 5, 6, 7])
        def p1_load(i):
            t0 = i * NT
            lo, hi = t0 - 16, t0 + NT + 16
            slo, shi = max(lo, 0), min(hi, S)
            xe, xeb = xes.next()
            c.dma("sp", xe[:, :, slo - lo:shi - lo], xT[:, slo:shi].rearrange("(k p) t -> p k t", p=128), [], [xeb], xeb)
            norm_sq(xe, xeb, WX, sq, sqb)
            return xe, xeb

        h_ap, hb = hs.next()
        nxt_x = p1_load(0)
        norm_rest(nxt_x[0], nxt_x[1], WX, 0, sq, sqb, rstd, rstdb, tts, h_ap, hb, psrot)
        for i in range(ntiles):
            t0 = i * NT
            mixed, mixb = mixeds.next()
            lo, hi = t0 - 16, t0 + NT + 16
            xe, xeb = nxt_x
            if i + 1 < ntiles:
                nxt_x = p1_load(i + 1)
            ph = psrot.next()
            for j in range(4):
                pa, pg = psrot.next(), psrot.next()
                mm_group(pa, 512, [w_in_sb[:, k, j * 128:(j + 1) * 128] for k in range(KC)], [h_ap[:, k, 0:512] for k in range(KC)], [hb])
                mm_group(pg, 512, [w_in_sb[:, k, 512 + j * 128:512 + (j + 1) * 128] for k in range(KC)], [h_ap[:, k, 0:512] for k in range(KC)], [hb])
                sgt, sgtb = sgts.next()
                c.op("act", lambda e, pg=pg, sgt=sgt: e.activation(out=sgt, in_=bank[pg], func=AF.Sigmoid), [bankb[pg]], [sgtb])
                c.op("dve", lambda e, pa=pa, sgt=sgt, j=j: e.tensor_tensor(out=a_t[:, j, 0:512], in0=bank[pa], in1=sgt, op=ALU.mult),
                     [bankb[pa], sgtb], [a_b])

            def fh(e, ph=ph, h_ap=h_ap):
                ins = None
                for j in range(4):
                    for g2 in range(2):
                        for k in range(KC):
                            ins = e.matmul(bank[ph][:, j * 64 + g2 * 32:j * 64 + g2 * 32 + 32],
                                           w_in_sb[:, k, g2 * 512 + j * 128:g2 * 512 + (j + 1) * 128], h_ap[:, k, 512:544],
                                           start=(k == 0), stop=(k == KC - 1))
                return ins
            c.op("pe", fh, [hb], [bankb[ph]])
            phv = bank[ph][:, 0:256].rearrange("p (j g n) -> p j g n", j=4, g=2)
            c.op("act", lambda e, phv=phv: e.activation(out=sgh, in_=phv[:, :, 1, :], func=AF.Sigmoid), [bankb[ph]], [sghb])
            c.op("dve", lambda e, phv=phv: e.tensor_tensor(out=a_t[:, :, 512:544], in0=phv[:, :, 0, :], in1=sgh, op=ALU.mult),
                 [bankb[ph], sghb], [a_b])
            if lo < 0:
                c.op("pool", lambda e: e.memset(a_t[:, :, 0:16], 0.0), [], [a_b])
            if hi > S:
                c.op("pool", lambda e: e.memset(a_t[:, :, 528:544], 0.0), [], [a_b])
            for g in range(4):
                pu = psrot.next()
                mm_group(pu, 512, [w_in_sb[:, k, 1024 + g * 128:1024 + (g + 1) * 128] for k in range(KC)], [h_ap[:, k, 16:528] for k in range(KC)], [hb])
                c.op("act", lambda e, pu=pu, g=g: e.activation(out=ug[:, g, :], in_=bank[pu], func=AF.Gelu), [bankb[pu]], [ugb])
            for n in range(4):
                pvb = psrot.next()
                mm_group(pvb, 512, [h_ap[:, k, 16 + n * 128:16 + (n + 1) * 128] for k in range(KC)], [w_in_sb[:, k, 1536:2048] for k in range(KC)], [hb])
                gv, gvb = gvs.next()
                c.op("act", lambda e, pvb=pvb, gv=gv: e.activation(out=gv, in_=bank[pvb], func=AF.Gelu), [bankb[pvb]], [gvb])
                c.op("dve", lambda e, gv=gv: e.bn_stats(out=st6[:, 0:6], in_=gv), [gvb], [st6b])
                c.op("dve", lambda e: e.bn_aggr(out=mv[:, 0:2], in_=st6[:, 0:6]), [st6b], [mvb])
                c.op("act", lambda e: e.activation(out=mv[:, 2:3], in_=mv[:, 1:2], func=AF.Ln, bias=LN_EPS), [mvb], [mvb])
                c.op("act", lambda e: e.activation(out=mv[:, 2:3], in_=mv[:, 2:3], func=AF.Exp, scale=-0.5), [mvb], [mvb])
                c.op("dve", lambda e, gv=gv, n=n: e.tensor_scalar(out=vn[:, n, :], in0=gv, scalar1=mv[:, 0:1], scalar2=mv[:, 2:3],
                                                               op0=ALU.subtract, op1=ALU.mult), [gvb, mvb], [vnb])
            if i + 1 < ntiles:
                norm_rest(nxt_x[0], nxt_x[1], WX, 0, sq, sqb, rstd, rstdb, tts, h_ap, hb, psrot)
            for cc in range(4):
                pc = psrot.next()
                mm_group(pc, 512, [dg[:, cc, w, :] for w in range(31)], [a_t[:, cc, w + 1:w + 513] for w in range(31)], [a_b, b_dg])
                cb_ap = pv[:, PV["convb"] + cc:PV["convb"] + cc + 1]
                c.op("act", lambda e, pc=pc, cc=cc, cb_ap=cb_ap: e.activation(out=yb[:, cc, :], in_=bank[pc], func=AF.Identity, bias=cb_ap),
                     [bankb[pc], b_pv], [ybb])
                c.op("act", lambda e, pc=pc, cc=cc, cb_ap=cb_ap: e.activation(out=ysq[:, cc, :], in_=bank[pc], func=AF.Square, bias=cb_ap),
                     [bankb[pc], b_pv], [ysqb])
            p1_, p2_ = psrot.next(), psrot.next()
            mm_group(p1_, 512, [ones_bf[:]] * 4, [yb[:, cc, :] for cc in range(4)], [ybb, b_ones])
            mm_group(p2_, 512, [ones_bf[:]] * 4, [ysq[:, cc, :] for cc in range(4)], [ysqb, b_ones])
            c.op("dve", lambda e, p1_=p1_: e.tensor_scalar(out=mean_t, in0=bank[p1_], scalar1=1.0 / 512, scalar2=None, op0=ALU.mult), [bankb[p1_]], [meanb])
            c.op("dve", lambda e: e.tensor_tensor(out=msq_t, in0=mean_t, in1=mean_t, op=ALU.mult), [meanb], [msqb])
            c.op("dve", lambda e, p2_=p2_: e.scalar_tensor_tensor(out=msq_t, in0=bank[p2_], scalar=1.0 / 512, in1=msq_t, op0=ALU.mult, op1=ALU.subtract),
                 [bankb[p2_], msqb], [msqb])
            c.op("act", lambda e: e.activation(out=rs2, in_=msq_t, func=AF.Ln, bias=LN_EPS), [msqb], [rs2b])
            c.op("act", lambda e: e.activation(out=rs2, in_=rs2, func=AF.Exp, scale=-0.5), [rs2b], [rs2b])
            for cc in range(4):
                dt_, dtb = dts.next()
                c.op("dve", lambda e, dt_=dt_, cc=cc: e.tensor_tensor(out=dt_, in0=yb[:, cc, :], in1=mean_t, op=ALU.subtract), [ybb, meanb], [dtb])
                c.op("dve", lambda e, dt_=dt_: e.tensor_tensor(out=dt_, in0=dt_, in1=rs2, op=ALU.mult), [dtb, rs2b], [dtb])
                g_ap = pv[:, PV["clng"] + cc:PV["clng"] + cc + 1]
                b_ap = pv[:, PV["clnb"] + cc:PV["clnb"] + cc + 1]
                c.op("act", lambda e, dt_=dt_, cc=cc, g_ap=g_ap, b_ap=b_ap, mixed=mixed: e.activation(out=mixed[:, cc, :], in_=dt_, func=AF.Silu, scale=g_ap, bias=b_ap),
                     [dtb, b_pv], [mixb])
            for g in range(4):
                pm = psrot.next()

                def fsp(e, pm=pm, g=g):
                    ins = None
                    for n in range(4):
                        ins = e.matmul(bank[pm][:, n * 128:(n + 1) * 128], vn[:, n, g * 128:(g + 1) * 128], sgw_bf[:, g, :], start=True, stop=True)
                    return ins
                c.op("pe", fsp, [vnb, b_sgwb], [bankb[pm]])
                tm, tmb = sgtmp.next()
                gg = pv[:, PV["sgg"] + g:PV["sgg"] + g + 1]
                c.op("dve", lambda e, pm=pm, tm=tm, g=g, gg=gg: e.scalar_tensor_tensor(
                    out=tm.rearrange("p (a b) -> p a b", a=4), in0=bank[pm].rearrange("p (a b) -> p a b", a=4), scalar=gg,
                    in1=Rt[:, g, :].unsqueeze(1).broadcast_to([128, 4, 128]), op0=ALU.mult, op1=ALU.add), [bankb[pm], b_R, b_pv], [tmb])
                c.op("pool", lambda e, tm=tm, g=g, mixed=mixed: e.tensor_tensor(out=mixed[:, 4 + g, :], in0=tm, in1=ug[:, g, :], op=ALU.mult), [tmb, ugb], [mixb])
            if i == 0:
                dbg_dump("a0", a_t, a_b, BF16)
                dbg_dump("yb0", yb, ybb, BF16)
                dbg_dump("mean0", mean_t, meanb)
                dbg_dump("var0", msq_t, msqb)
                dbg_dump("rs20", rs2, rs2b)
                dbg_dump("h0", h_ap, hb, BF16)
                dbg_dump("rstd0", rstd, rstdb)
            if i == 1:
                dbg_dump("u1", ug, ugb, BF16)
                dbg_dump("vn1", vn, vnb, BF16)
                dbg_dump("R", Rt, b_R)
                dbg_dump("tm1", tm, tmb)
            c.dma(STORE_Q, scr["mixT"][:, t0:t0 + NT].rearrange("(k p) t -> p k t", p=128), mixed, [mixb], [], mixb)
        c.barrier()
        c.release([b for (_, b) in xes.items] + [b for (_, b) in mixeds.items])

    def mlp_pass(s, layer, half, src_d, res_d, out_d, final=False, pre=None):
        arena.reset()
        w1h, _ = arena.alloc((8, 2048), BF16)
        w2h, _ = arena.alloc((16, 1024), BF16)
        if pre is not None:
            wpre_sb, _ = arena.alloc((8, 1024), BF16)
        mark = arena.off
        stages = [arena.alloc((2048,), F32) for _ in range(3)]
        load_w_bf16(w1h, w1_d[layer][:, half * 2048:(half + 1) * 2048], 8, 2048, stages)
        load_w_bf16(w2h, w2_d[layer][half * 2048:(half + 1) * 2048, :], 16, 1024, stages)
        if pre is not None:
            load_w_bf16(wpre_sb, pre[4], 8, 1024, stages)
        c.barrier()
        c.release([st[1] for st in stages])
        arena.off = mark
        xss = Rot([arena.alloc((8, 512), F32) for _ in range(2)])
        rch = Rot([arena.alloc((512,), F32) for _ in range(4)])
        sq, sqb = arena.alloc((8, 512), BF16)
        rstd, rstdb = arena.alloc((512,), F32)
        tts = Rot([arena.alloc((512,), F32) for _ in range(3)])
        hs = Rot([arena.alloc((8, 512), BF16) for _ in range(2)])
        rts = Rot([arena.alloc((512,), BF16) for _ in range(3)])
        hid, hidb = arena.alloc((16, 512), BF16)
        if final:
            xof, xofb = arena.alloc((8, 512), F32)
            sq2, sq2b = arena.alloc((8, 512), BF16)
            rstd2, rstd2b = arena.alloc((512,), F32)
        else:
            och = Rot([arena.alloc((512,), F32) for _ in range(4)])
        if pre is not None:
            ots = Rot([arena.alloc((8, 512), BF16) for _ in range(1)])
        psrot = Rot([0, 1, 2, 3, 4, 5, 6, 7])
        tr = lambda d_, t0: d_[:, t0:t0 + NT].rearrange("(k p) t -> p k t", p=128)
        ch = lambda d_, m, t0: d_[m * 128:(m + 1) * 128, t0:t0 + NT]
        def stage_a1(i):
            t0 = i * NT
            xs, xsb = xss.next()
            if pre is None:
                c.dma("sp", xs, tr(src_d, t0), [], [xsb], xsb)
            else:
                o_d, xres_d, xout_d, sa, _w = pre
                ot, otb = ots.next()
                c.dma("sp", ot, tr(o_d, t0), [], [otb], otb)
                for m in range(KC):
                    rc, rcb = rch.next()
                    c.dma("sp", rc, ch(xres_d, m, t0), [], [rcb], rcb)
                    po = psrot.next()
                    mm_group(po, 512, [wpre_sb[:, k, m * 128:(m + 1) * 128] for k in range(KC)], [ot[:, k, :] for k in range(KC)], [otb])
                    c.op("dve", lambda e, po=po, m=m, xs=xs, rc=rc: e.scalar_tensor_tensor(out=xs[:, m, :], in0=bank[po], scalar=GATE(sa, m), in1=rc,
                                                                                       op0=ALU.mult, op1=ALU.add), [bankb[po], rcb, b_modv], [xsb])
                c.dma(STORE_Q, tr(xout_d, t0), xs, [xsb], [], xsb)
            h_ap, hb = hs.next()
            return {"t0": t0, "xs": xs, "xsb": xsb, "h": h_ap, "hb": hb}

        def stage_a2(st, mid=None):
            norm_h(st["xs"], st["xsb"], 512, s, sq, sqb, rstd, rstdb, tts, st["h"], st["hb"], psrot, mid=mid)

        def stage_b(st):
            h_ap, hb = st["h"], st["hb"]
            for j in range(16):
                pj = psrot.next()
                mm_group(pj, 512, [w1h[:, k, j * 128:(j + 1) * 128] for k in range(KC)], [h_ap[:, k, :] for k in range(KC)], [hb])
                rt, rtb = rts.next()
                c.op("act", lambda e, pj=pj, rt=rt: e.activation(out=rt, in_=bank[pj], func=AF.Relu), [bankb[pj]], [rtb])
                c.op("dve", lambda e, pj=pj, rt=rt, j=j: e.scalar_tensor_tensor(out=hid[:, j, :], in0=bank[pj], scalar=0.0, in1=rt, op0=ALU.max, op1=ALU.mult),
                     [bankb[pj], rtb], [hidb])

        def stage_c(st, m0, m1):
            t0, xs, xsb = st["t0"], st["xs"], st["xsb"]
            for m in range(m0, m1):
                if res_d is not None:
                    rc, rcb = rch.next()
                    c.dma("sp", rc, ch(res_d, m, t0), [], [rcb], rcb)
                else:
                    rc, rcb = xs[:, m, :], xsb
                po = psrot.next()
                mm_group(po, 512, [w2h[:, j, m * 128:(m + 1) * 128] for j in range(16)], [hid[:, j, :] for j in range(16)], [hidb])
                if final:
                    oc, ocb = xof[:, m, :], xofb
                else:
                    oc, ocb = och.next()
                c.op("dve", lambda e, po=po, m=m, oc=oc, rc=rc: e.scalar_tensor_tensor(out=oc, in0=bank[po], scalar=GATE(s, m), in1=rc,
                                                                                   op0=ALU.mult, op1=ALU.add), [bankb[po], rcb, b_modv], [ocb])
                if not final:
                    c.dma(STORE_Q, ch(out_d, m, t0), oc, [ocb], [], ocb)

        def stage_fin(st):
            t0 = st["t0"]
            c.op("act", lambda e: e.activation(out=sq2, in_=xof, func=AF.Square), [xofb], [sq2b])
            pf = psrot.next()
            mm_group(pf, 512, [ones_bf[:]] * KC, [sq2[:, k, :] for k in range(KC)], [sq2b, b_ones])
            rstd_from_psum(bank[pf], rstd2, 512, 1.0 / D, RMS_EPS, [bankb[pf]], [rstd2b])
            for m in range(KC):
                fg = pv[:, PV["fing"] + m:PV["fing"] + m + 1]
                c.op("dve", lambda e, m=m, fg=fg: e.scalar_tensor_tensor(out=xof[:, m, :], in0=xof[:, m, :], scalar=fg, in1=rstd2, op0=ALU.mult, op1=ALU.mult),
                     [xofb, rstd2b, b_pv], [xofb])
            c.dma(STORE_Q, tr(out_d, t0), xof, [xofb], [], xofb)

        cur = stage_a1(0)
        stage_a2(cur)
        for i in range(ntiles):
            stage_b(cur)
            nxt = None
            if i + 1 < ntiles:
                nxt = stage_a1(i + 1)
                stage_a2(nxt, mid=lambda cur=cur: stage_c(cur, 0, 4))
            else:
                stage_c(cur, 0, 4)
            stage_c(cur, 4, KC)
            if final:
                stage_fin(cur)
            cur = nxt
        c.barrier()
        rel = [b for (_, b) in xss.items] + [b for (_, b) in rch.items]
        if final:
            rel.append(xofb)
        else:
            rel += [b for (_, b) in och.items]
        if pre is not None:
            rel += [b for (_, b) in ots.items]
        c.release(rel)

    if "p2a" in phases:
        mlp_pass(1, 0, 0, None, None, scr["xaT"], pre=(scr["mixT"], xT, scr["x1T"], 0, w_out_d))
    if "p2b" in phases:
        mlp_pass(1, 0, 1, scr["x1T"], scr["xaT"], scr["x2T"])

    if "p3" in phases:
        arena.reset()
        wq, _ = arena.alloc((8, 3072), BF16)
        stages = [arena.alloc((2048,), F32) for _ in range(3)]
        load_w_bf16(wq, wqkv_d, 8, 3072, stages)
        c.barrier()
        c.release([st[1] for st in stages])
        xss = Rot([arena.alloc((8, 512), F32) for _ in range(2)])
        sq, sqb = arena.alloc((8, 512), BF16)
        rstd, rstdb = arena.alloc((512,), F32)
        tts = Rot([arena.alloc((512,), F32) for _ in range(3)])
        hs = Rot([arena.alloc((8, 512), BF16) for _ in range(2)])
        qos = Rot([arena.alloc((8, 512), BF16) for _ in range(2)])
        kos = Rot([arena.alloc((8, 512), BF16) for _ in range(2)])
        vos = Rot([arena.alloc((8, 4, 128), BF16) for _ in range(2)])
        psrot = Rot([0, 1, 2, 3, 4, 5, 6, 7])
        tr = lambda d_, t0: d_[:, t0:t0 + NT].rearrange("(k p) t -> p k t", p=128)
        def p3_a1(i):
            t0 = i * NT
            xs, xsb = xss.next()
            c.dma("sp", xs, tr(scr["x2T"], t0), [], [xsb], xsb)
            h_ap, hb = hs.next()
            return {"i": i, "t0": t0, "xs": xs, "xsb": xsb, "h": h_ap, "hb": hb}

        def p3_b(st):
            t0, h_ap, hb = st["t0"], st["h"], st["hb"]
            qo, qob = qos.next()
            ko, kob = kos.next()
            for m in range(16):
                pq = psrot.next()
                mm_group(pq, 512, [wq[:, k, m * 128:(m + 1) * 128] for k in range(KC)], [h_ap[:, k, :] for k in range(KC)], [hb])
                if m < 8:
                    c.op("act", lambda e, pq=pq, m=m, qo=qo: e.activation(out=qo[:, m, :], in_=bank[pq], func=AF.Copy, scale=0.125), [bankb[pq]], [qob])
                else:
                    c.op("dve", lambda e, pq=pq, m=m, ko=ko: e.tensor_copy(out=ko[:, m - 8, :], in_=bank[pq]), [bankb[pq]], [kob])
            c.dma(STORE_Q, tr(scr["qT"], t0), qo, [qob], [], qob)
            c.dma(STORE_Q, tr(scr["kT"], t0), ko, [kob], [], kob)
            st["vo"] = vos.next()

        def p3_c(st, n0, n1):
            i, h_ap, hb = st["i"], st["h"], st["hb"]
            vo, vob = st["vo"]
            for n in range(n0, n1):
                for hf in range(2):
                    pvv = psrot.next()
                    mm_group(pvv, 512, [h_ap[:, k, n * 128:(n + 1) * 128] for k in range(KC)],
                             [wq[:, k, 2048 + hf * 512:2048 + (hf + 1) * 512] for k in range(KC)], [hb])
                    src = bank[pvv].rearrange("p (a b) -> p a b", a=4)
                    if hf == 0:
                        c.op("act", lambda e, src=src, vo=vo, n=n, hf=hf: e.activation(out=vo[:, hf * 4:(hf + 1) * 4, n, :], in_=src, func=AF.Copy), [bankb[pvv]], [vob])
                    else:
                        c.op("dve", lambda e, src=src, vo=vo, n=n, hf=hf: e.tensor_copy(out=vo[:, hf * 4:(hf + 1) * 4, n, :], in_=src), [bankb[pvv]], [vob])
            if n1 == 4:
                c.dma(STORE_Q, scr["Vs"][:, :, 4 * i:4 * i + 4, :].rearrange("h p n e -> p h n e"), vo, [vob], [], vob)

        cur = p3_a1(0)
        norm_h(cur["xs"], cur["xsb"], 512, 2, sq, sqb, rstd, rstdb, tts, cur["h"], cur["hb"], psrot)
        for i in range(ntiles):
            p3_b(cur)
            nxt = None
            if i + 1 < ntiles:
                nxt = p3_a1(i + 1)
                norm_h(nxt["xs"], nxt["xsb"], 512, 2, sq, sqb, rstd, rstdb, tts, nxt["h"], nxt["hb"], psrot,
                       mid=lambda cur=cur: p3_c(cur, 0, 2))
            else:
                p3_c(cur, 0, 2)
            p3_c(cur, 2, 4)
            cur = nxt
        c.barrier()
        c.release([b for (_, b) in xss.items] + [b for (_, b) in qos.items] + [b for (_, b) in kos.items] + [b for (_, b) in vos.items])

    if "p4" in phases:
        arena.reset()
        NKC = S // 128
        kTs = [arena.alloc((2, S), BF16) for _ in range(2)]
        qTs = [arena.alloc((S,), BF16) for _ in range(2)]
        Vhs = [arena.alloc((NKC, 128), BF16) for _ in range(2)]
        bths = [arena.alloc((NDELTA, 512), BF16) for _ in range(2)]
        btst = Rot([arena.alloc((512,), F32) for _ in range(2)])
        pts = Rot([arena.alloc((2, 512), BF16) for _ in range(4)])
        osqs = Rot([arena.alloc((512,), BF16) for _ in range(2)])
        ons = Rot([arena.alloc((512,), BF16) for _ in range(2)])
        for (kt_, kb_) in kTs:
            c.op("pool", lambda e, kt_=kt_: e.memset(kt_, 0.0), [], [kb_])
        srot = Rot([0, 1])
        Sb = [Buf(), Buf()]
        O1, O2, S1, S2 = 4, 5, 6, 7
        kb1s = [Buf(), Buf()]
        cb0 = BC["cb"]
        FIN_DELAY = 14
        pending = []

        def emit_qk(it):
            if "fin" in it:
                si = srot.next()
                it["si"] = si
                Sp = pp[si]
                osq, osqb = it["fin"][6]
                c.op("pe", lambda e: e.matmul(Sp[:, 0, :], ones_bf[:], osq, start=True, stop=True), [osqb, b_ones], [Sb[si]])
                return
            hh, qt, kc, bufs = it["hh"], it["qt"], it["kc"], it["bufs"]
            kTh, kb, kb1, qTh, qb, Vh, vb, bth, bb = bufs
            q0 = qt * NT
            delta = kc - 4 * qt
            near = (-1 <= delta <= NDELTA - 2)
            si = srot.next()
            Sp = pp[si]
            it["si"], it["near"], it["delta"] = si, near, delta

            def fqk(e):
                e.matmul(Sp[:, 0, :], kTh[:, 0, kc * 128:(kc + 1) * 128], qTh[:, q0:q0 + NT], start=True, stop=not near)
                ins = e.matmul(Sp[:, 1, :], kTh[:, 1, kc * 128:(kc + 1) * 128], qTh[:, q0:q0 + NT], start=True, stop=not near)
                if near:
                    e.matmul(Sp[:, 0, :], ident_bf[:], bth[:, delta + 1, :], start=False, stop=True)
                    ins = e.matmul(Sp[:, 1, :], ident_bf[:], bth[:, delta + 1, :], start=False, stop=True)
                return ins
            c.op("pe", fqk, [kb, kb1, qb, bb, b_identb] if near else [kb, kb1, qb], [Sb[si]])

        def emit_exp(it):
            if "fin" in it:
                (s1, s1b), (s2, s2b), (t1, t1b), (t2, t2b), (ot, otb), (rs, rsb), _osq = it["fin"]
                hh, qt = it["hh"], it["qt"]
                q0 = qt * NT
                Sp = pp[it["si"]]
                c.op("act", lambda e: e.activation(out=rs, in_=Sp[:, 0, :], func=AF.Ln, scale=1.0 / 128, bias=LN_EPS), [Sb[it["si"]]], [rsb])
                c.op("act", lambda e: e.activation(out=rs, in_=rs, func=AF.Exp, scale=-0.5), [rsb], [rsb])
                on, onb = ons.next()
                c.op("dve", lambda e: e.scalar_tensor_tensor(out=on, in0=ot, scalar=SUBLNG, in1=rs, op0=ALU.mult, op1=ALU.mult), [otb, rsb, b_small], [onb])
                c.dma(STORE_Q, scr["oT"][hh * 128:(hh + 1) * 128, q0:q0 + NT], on, [onb], [], onb)
                return
            hh, qt, kc, bufs = it["hh"], it["qt"], it["kc"], it["bufs"]
            kTh, kb, kb1, qTh, qb, Vh, vb, bth, bb = bufs
            si, near, delta = it["si"], it["near"], it["delta"]
            Sp = pp[si]
            pt, ptb = pts.next()
            if near:
                bias = 0.0
            else:
                col = cb0 + hh * 2 + (0 if delta < 0 else 1)
                bias = bcs[:, col:col + 1]
            c.op("act", lambda e: e.activation(out=pt.rearrange("p a b -> p (a b)"), in_=Sp[:].rearrange("p a b -> p (a b)"),
                                               func=AF.Exp, bias=bias), [Sb[si], b_bc], [ptb])
            it["pt"] = (pt, ptb)

        def emit_pv(it):
            if "fin" in it:
                return
            hh, qt, kc, bufs = it["hh"], it["qt"], it["kc"], it["bufs"]
            kTh, kb, kb1, qTh, qb, Vh, vb, bth, bb = bufs
            pt, ptb = it["pt"]

            if kc == 0:
                c.op("dve", lambda e: e.tensor_copy(out=bank[S1], in_=pt[:, 0, :]), [ptb], [bankb[S1]])
            else:
                c.op("dve", lambda e: e.tensor_tensor(out=bank[S1], in0=bank[S1], in1=pt[:, 0, :], op=ALU.add), [ptb, bankb[S1]], [bankb[S1]])

            def fpv(e):
                st_, sp_ = (kc == 0), (kc == NKC - 1)
                e.matmul(bank[O1], Vh[:, kc, :], pt[:, 0, :], start=st_, stop=sp_)
                e.matmul(bank[O2], Vh[:, kc, :], pt[:, 1, :], start=st_, stop=sp_)
                return e.matmul(bank[S2], ones_bf[:], pt[:, 1, :], start=st_, stop=sp_)
            c.op("pe", fpv, [ptb, vb, b_ones], [bankb[O1], bankb[O2], bankb[S2]])
            if kc == NKC - 1:
                emit_fin_a(hh, qt)

        fins = Rot([[arena.alloc((512,), F32) for _ in range(6)] for _ in range(2)])

        def emit_fin_a(hh, qt):
            q0 = qt * NT
            fs = list(fins.next())
            (s1, s1b), (s2, s2b), (t1, t1b), (t2, t2b), (ot, otb), (rs, rsb) = fs
            osq, osqb = osqs.next()
            c.op("dve", lambda e: e.tensor_copy(out=osq, in_=bank[S1]), [bankb[S1]], [osqb])
            c.op("dve", lambda e: e.tensor_copy(out=t1, in_=bank[O1]), [bankb[O1]], [t1b])
            c.op("dve", lambda e: e.tensor_copy(out=t2, in_=bank[O2]), [bankb[O2]], [t2b])
            c.op("pe", lambda e: e.matmul(bank[S1], ones_bf[:], osq, start=True, stop=True), [osqb, b_ones], [bankb[S1]])
            c.op("act", lambda e: e.activation(out=s1, in_=bank[S1], func=AF.Ln), [bankb[S1]], [s1b])
            c.op("act", lambda e: e.activation(out=s1, in_=s1, func=AF.Exp, scale=-1.0), [s1b], [s1b])
            c.op("act", lambda e: e.activation(out=s2, in_=bank[S2], func=AF.Ln), [bankb[S2]], [s2b])
            c.op("act", lambda e: e.activation(out=s2, in_=s2, func=AF.Exp, scale=-1.0), [s2b], [s2b])
            c.op("dve", lambda e: e.tensor_tensor(out=t1, in0=t1, in1=s1, op=ALU.mult), [t1b, s1b], [t1b])
            c.op("dve", lambda e: e.tensor_tensor(out=t2, in0=t2, in1=s2, op=ALU.mult), [t2b, s2b], [t2b])
            c.op("dve", lambda e: e.scalar_tensor_tensor(out=ot, in0=t2, scalar=NEGLAM, in1=t1, op0=ALU.mult, op1=ALU.add), [t1b, t2b, b_small], [otb])
            c.op("dve", lambda e: e.tensor_tensor(out=osq, in0=ot, in1=ot, op=ALU.mult), [otb], [osqb])
            pending.append([FIN_DELAY, {"fin": fs + [(osq, osqb)], "hh": hh, "qt": qt}])

        inflight = []

        def step(it):
            if len(inflight) >= 2:
                emit_exp(inflight[0])
            emit_qk(it)
            if len(inflight) >= 2:
                emit_pv(inflight.pop(0))
            inflight.append(it)

        for hh in range(nheads):
            kTh, kb = kTs[hh % 2]
            qTh, qb = qTs[hh % 2]
            Vh, vb = Vhs[hh % 2]
            bth, bb = bths[hh % 2]
            kb1 = kb1s[hh % 2]
            c.dma("sp", kTh[0:64, 0, :], scr["kT"][hh * 128:hh * 128 + 64, :], [], [kb], kb)
            c.dma("sp", kTh[64:128, 1, :], scr["kT"][hh * 128 + 64:(hh + 1) * 128, :], [kb], [kb1], kb1)
            c.dma("sp", qTh, scr["qT"][hh * 128:(hh + 1) * 128, :], [], [qb], qb)
            c.dma("sp", Vh, scr["Vs"][hh], [], [vb], vb)
            for di in range(NDELTA):
                st_, stb_ = btst.next()
                c.dma("sp", st_, bt_d[hh, di], [], [stb_], stb_)
                c.op("pool", lambda e, st_=st_, bth=bth, di=di: e.tensor_copy(out=bth[:, di, :], in_=st_), [stb_], [bb])
            bufs = (kTh, kb, kb1, qTh, qb, Vh, vb, bth, bb)
            for qt in range(ntiles):
                for kc in range(NKC):
                    step({"hh": hh, "qt": qt, "kc": kc, "bufs": bufs})
                    for p_ in pending:
                        p_[0] -= 1
                    if pending and pending[0][0] <= 0:
                        step(pending.pop(0)[1])
        while True:
            if pending:
                step(pending.pop(0)[1])
            elif inflight:
                it = inflight.pop(0)
                emit_exp(it)
                emit_pv(it)
            else:
                break
        c.barrier()
        c.release(kb1s + [b for (_, b) in kTs + qTs + Vhs] + [b for (_, b) in btst.items] + [b for (_, b) in ons.items])

    if "p6a" in phases:
        mlp_pass(3, 1, 0, None, None, scr["xbT"], pre=(scr["oT"], scr["x2T"], scr["x3T"], 2, wao_d))
    if "p6b" in phases:
        mlp_pass(3, 1, 1, scr["x3T"], scr["xbT"], outT, final=True)

    c.barrier()
    c.emit()
    return nc


def _t5_bucket_np(rel):
    import jax
    import jax.numpy as jnp
    nb = 16
    max_exact = 8
    with jax.default_device(jax.devices("cpu")[0]):
        rel = jnp.asarray(rel, dtype=jnp.int32)
        ret = jnp.where(rel > 0, nb, 0).astype(jnp.int32)
        n = jnp.abs(rel)
        nf = jnp.maximum(n, 1).astype(jnp.float32)
        large = max_exact + (jnp.log(nf / max_exact) / math.log(128 / max_exact) * (nb - max_exact)).astype(jnp.int32)
        large = jnp.minimum(large, nb - 1)
        return np.asarray(ret + jnp.where(n < max_exact, n, large))


def _fm(v, ncol):
    return np.ascontiguousarray(np.asarray(v, np.float32).reshape(ncol, 128).T)


def prep_shared(inp):
    g = lambda k: np.asarray(inp[k], np.float32)
    sh = {}
    sh["modw0"] = np.ascontiguousarray(g("mix_mod_w")[0])
    sh["modw1"] = np.ascontiguousarray(g("mlp_mod_w")[0])
    sh["modw2"] = np.ascontiguousarray(g("mix_mod_w")[1])
    sh["modw3"] = np.ascontiguousarray(g("mlp_mod_w")[1])
    sh["w_in"] = np.ascontiguousarray(g("ab_w_in")[0])
    sh["w_out"] = np.ascontiguousarray(g("ab_w_out")[0])
    sh["wqkv"] = np.ascontiguousarray(g("attn_w_qkv")[0])
    sh["wao"] = np.ascontiguousarray(g("attn_w_out")[0])
    for i in range(2):
        sh["w1_%d" % i] = np.ascontiguousarray(g("mlp_w1")[i])
        sh["w2_%d" % i] = np.ascontiguousarray(g("mlp_w2")[i])
    sh["ident"] = np.eye(128, dtype=np.float32)
    sh["sgwT"] = np.ascontiguousarray(np.transpose(g("sg_w")[0], (2, 0, 1)).reshape(128, 512))
    rel = np.arange(-(S - 1), S, dtype=np.int32)
    bucket = _t5_bucket_np(rel)
    tab = g("rel_bias_table")
    assert len(set(bucket[: S - 1 - 128].tolist())) == 1 and len(set(bucket[S - 1 + 129:].tolist())) == 1
    bt = np.empty((H, NDELTA, 128, 512), np.float32)
    ii = np.arange(128)[:, None]
    jj = np.arange(512)[None, :]
    for di in range(NDELTA):
        relm = 128 * (di - 1) + ii - jj
        bk = bucket[relm + S - 1]
        bt[:, di] = np.transpose(tab[bk], (2, 0, 1))
    sh["bt"] = bt
    cb = np.empty((H, 2), np.float32)
    cb[:, 0] = tab[bucket[0]]
    cb[:, 1] = tab[bucket[-1]]
    pvs = np.zeros((128, NPV), np.float32)

    def put(name, v, ncol):
        pvs[:, PV[name]:PV[name] + ncol] = _fm(v, ncol)
    put("mixg0", g("mix_norm_g")[0], 8)
    put("mlpg0", g("mlp_norm_g")[0], 8)
    put("mixg1", g("mix_norm_g")[1], 8)
    put("mlpg1", g("mlp_norm_g")[1], 8)
    put("fing", g("final_norm_g"), 8)
    put("modb0", g("mix_mod_b")[0], 24)
    put("modb1", g("mlp_mod_b")[0], 24)
    put("modb2", g("mix_mod_b")[1], 24)
    put("modb3", g("mlp_mod_b")[1], 24)
    put("convb", g("conv_b")[0], 4)
    put("clng", g("conv_ln_g")[0], 4)
    put("clnb", g("conv_ln_b")[0], 4)
    put("sgg", g("sg_ln_g")[0], 4)
    put("sublng", g("subln_g")[0], 1)
    cw = g("conv_w")[0]
    pvs[:, PV["convw"]:PV["convw"] + 124] = np.transpose(cw.reshape(31, 4, 128), (2, 1, 0)).reshape(128, 124)
    bcv = np.zeros((128, NBC), np.float32)
    bcv[:, 0:64] = g("lam_q1")[0][None]
    bcv[:, 64:128] = g("lam_k1")[0][None]
    bcv[:, 128:192] = g("lam_q2")[0][None]
    bcv[:, 192:256] = g("lam_k2")[0][None]
    bcv[:, BC["sgbb"]:BC["sgbb"] + 512] = g("sg_b")[0].reshape(1, 512)
    bcv[:, BC["bmat"]:BC["bmat"] + 512] = g("sg_ln_b")[0].reshape(1, 512)
    bcv[:, BC["cb"]:BC["cb"] + 16] = cb.reshape(1, 16)
    sh["bc"] = bcv
    return sh, pvs


def core_inputs(inp, b, sh, pvs):
    m = dict(sh)
    pvb = pvs.copy()
    pvb[:, PV["c"]:PV["c"] + 8] = _fm(np.asarray(inp["c"], np.float32)[b], 8)
    m["pv"] = pvb
    m["xT"] = np.ascontiguousarray(np.asarray(inp["x"], np.float32)[b].T)
    return m


_NC_CACHE = {}


def kernel(**inputs):
    sh, pvs = prep_shared(inputs)
    B = np.asarray(inputs["x"]).shape[0]
    if "full" not in _NC_CACHE:
        _NC_CACHE["full"] = build()
    nc = _NC_CACHE["full"]
    in_maps = [core_inputs(inputs, b, sh, pvs) for b in range(B)]
    res = run_bass_kernel_spmd(nc, in_maps, core_ids=list(range(B)))
    out = np.empty((B, S, D), np.float32)
    for b in range(B):
        out[b] = res.results[b]["outT"].T
    return out
```
